# Optimizing a Trainium2 kernel written in Bass

```python
import jax, jax.numpy as jnp
from jax import lax
import numpy as np

D_MODEL = 2048
BATCH = 1
SEQ = 8192
DEPTH = 2

GRID_W = 64
CTX_LEN = 256
N_HEADS_NA = 8
HEAD_DIM = 128
D_NA = N_HEADS_NA * HEAD_DIM
WIN_R = 8
WIN_C = 16
N_FOURIER_GROUPS = 4
FOURIER_GROUP = 128
D_FOURIER = N_FOURIER_GROUPS * FOURIER_GROUP
D_CONV = 512
CONV_WIDTH = 31
D_FF = 5632
D_IN = 3 * D_NA + D_FOURIER + 2 * D_CONV
N_BRANCH = 3
N_MOD = 9
ALPHA = (2 * DEPTH) ** 0.25
BETA = (8 * DEPTH) ** -0.25
LN_EPS = 1e-5
NEG_INF = -1e30

kernel_name = "hybrid_na_fnet_conformer_diffusion_block"


def layer_norm(x, g, b):
    xf = x.astype(jnp.float32)
    mu = xf.mean(-1, keepdims=True)
    var = jnp.square(xf - mu).mean(-1, keepdims=True)
    y = (xf - mu) * lax.rsqrt(var + LN_EPS)
    return (y * g.astype(jnp.float32) + b.astype(jnp.float32)).astype(x.dtype)


def modulate(x, shift, scale):
    return x * (1 + scale) + shift


def swiglu(h, w1, w3, w2):
    return (jax.nn.silu(h @ w1) * (h @ w3)) @ w2


def ffn_sublayer(z, m, j, w1, w3, w2, g, b):
    h = modulate(z, m[:, :, 3 * j], m[:, :, 3 * j + 1])
    y = swiglu(h, w1, w3, w2)
    return layer_norm(ALPHA * z + 0.5 * m[:, :, 3 * j + 2] * y, g, b)


def split_in(z):
    b_, n, _ = z.shape
    heads = lambda t: t.reshape(b_, n, N_HEADS_NA, HEAD_DIM)
    q = heads(z[..., 0:D_NA])
    k = heads(z[..., D_NA:2 * D_NA])
    v = heads(z[..., 2 * D_NA:3 * D_NA])
    o = 3 * D_NA
    f = z[..., o:o + D_FOURIER]
    o += D_FOURIER
    a = z[..., o:o + D_CONV]
    g = z[..., o + D_CONV:o + 2 * D_CONV]
    return q, k, v, f, a, g


def fourier_mix(f):
    b_, n, _ = f.shape
    fg = f.astype(jnp.float32).reshape(b_, n, N_FOURIER_GROUPS, FOURIER_GROUP)
    out = jnp.fft.fft2(fg, axes=(1, 3), norm="ortho").real
    return out.reshape(b_, n, D_FOURIER).astype(f.dtype)


def conv_module(a, g, w_dw, b_dw, ln_g, ln_b):
    u = a * jax.nn.sigmoid(g)
    y = lax.conv_general_dilated(
        u, w_dw[:, None, :], window_strides=(1,),
        padding=[(CONV_WIDTH // 2, CONV_WIDTH // 2)],
        dimension_numbers=("NWC", "WIO", "NWC"),
        feature_group_count=D_CONV)
    y = y + b_dw
    return jax.nn.silu(layer_norm(y, ln_g, ln_b))


def neighbourhood_attention(q, k, v, k_ctx, v_ctx, rpb):
    b_, n, h_, dh = q.shape
    rows = n // GRID_W
    wr = min(WIN_R, rows)
    scale = dh ** -0.5

    def grid(t):
        return t.reshape(b_, rows, GRID_W, h_, dh).transpose(0, 3, 1, 2, 4)

    qg, kg, vg = grid(q), grid(k), grid(v)
    r = jnp.arange(rows)
    r0 = jnp.clip(r - wr // 2, 0, rows - wr)
    row_idx = r0[:, None] + jnp.arange(wr)[None, :]
    n_band = wr * GRID_W
    k_band = kg[:, :, row_idx].reshape(b_, h_, rows, n_band, dh)
    v_band = vg[:, :, row_idx].reshape(b_, h_, rows, n_band, dh)

    cq = jnp.arange(GRID_W)
    c0 = jnp.clip(cq - WIN_C // 2, 0, GRID_W - WIN_C)
    col_ok = (cq[None, :] >= c0[:, None]) & (cq[None, :] < c0[:, None] + WIN_C)
    dc_i = jnp.clip(cq[None, :] - cq[:, None], -(WIN_C - 1), WIN_C - 1) + WIN_C - 1
    dr_i = row_idx - r[:, None] + WIN_R - 1
    bias = rpb.astype(jnp.float32)[:, dr_i[:, :, None, None], dc_i[None, None, :, :]]
    bias = jnp.where(col_ok[None, None, None], bias, NEG_INF)
    bias = bias.transpose(0, 1, 3, 2, 4).reshape(h_, rows, GRID_W, n_band)

    s_lat = jnp.einsum('bhrqd,bhrkd->bhrqk', qg, k_band,
                       preferred_element_type=jnp.float32) * scale + bias[None]
    s_ctx = jnp.einsum('bhrqd,bchd->bhrqc', qg, k_ctx,
                       preferred_element_type=jnp.float32) * scale
    p = jax.nn.softmax(jnp.concatenate([s_lat, s_ctx], axis=-1), axis=-1)
    p_lat = p[..., :n_band].astype(v.dtype)
    p_ctx = p[..., n_band:].astype(v.dtype)
    o = (jnp.einsum('bhrqk,bhrkd->bhrqd', p_lat, v_band)
         + jnp.einsum('bhrqc,bchd->bhrqd', p_ctx, v_ctx))
    return o.transpose(0, 2, 3, 1, 4).reshape(b_, n, h_ * dh)


def context_attention(q, k, v):
    b_, n, h_, dh = q.shape
    s = jnp.einsum('bqhd,bkhd->bhqk', q, k, preferred_element_type=jnp.float32) * (dh ** -0.5)
    p = jax.nn.softmax(s, axis=-1).astype(v.dtype)
    return jnp.einsum('bhqk,bkhd->bqhd', p, v).reshape(b_, n, h_ * dh)


def merge_branches(h, y_na, y_f, y_c, w_gate, b_gate, w_b_na, w_b_f, w_b_c, w_o):
    b_, n, _ = h.shape
    gates = jax.nn.sigmoid(h @ w_gate + b_gate).reshape(b_, n, N_BRANCH, D_MODEL)
    m = (gates[:, :, 0] * (y_na @ w_b_na)
         + gates[:, :, 1] * (y_f @ w_b_f)
         + gates[:, :, 2] * (y_c @ w_b_c))
    return m @ w_o


def setup_inputs(seed: int = 0) -> dict:
    key = jax.random.key(seed)
    ks = jax.random.split(key, 32)
    nrm = lambda k, shape, s: jax.random.normal(k, shape, jnp.float32) * s
    L = DEPTH
    return {
        "x": nrm(ks[0], (BATCH, SEQ, D_MODEL), 1.0),
        "c": nrm(ks[1], (BATCH, D_MODEL), 1.0),
        "ctx": nrm(ks[2], (BATCH, CTX_LEN, D_MODEL), 1.0),
        "c_ctx": nrm(ks[3], (D_MODEL,), 1.0),
        "w_ada": nrm(ks[4], (L, D_MODEL, N_MOD * D_MODEL), 0.5 * D_MODEL ** -0.5),
        "b_ada": nrm(ks[5], (L, N_MOD * D_MODEL), 0.02),
        "ffn1_w1": nrm(ks[6], (L, D_MODEL, D_FF), D_MODEL ** -0.5),
        "ffn1_w3": nrm(ks[7], (L, D_MODEL, D_FF), D_MODEL ** -0.5),
        "ffn1_w2": nrm(ks[8], (L, D_FF, D_MODEL), BETA * D_FF ** -0.5),
        "ffn2_w1": nrm(ks[9], (L, D_MODEL, D_FF), D_MODEL ** -0.5),
        "ffn2_w3": nrm(ks[10], (L, D_MODEL, D_FF), D_MODEL ** -0.5),
        "ffn2_w2": nrm(ks[11], (L, D_FF, D_MODEL), BETA * D_FF ** -0.5),
        "w_in": nrm(ks[12], (L, D_MODEL, D_IN), D_MODEL ** -0.5),
        "w_dw": nrm(ks[13], (L, CONV_WIDTH, D_CONV), CONV_WIDTH ** -0.5),
        "b_dw": nrm(ks[14], (L, D_CONV), 0.02),
        "conv_ln_g": 1.0 + nrm(ks[15], (L, D_CONV), 0.02),
        "conv_ln_b": nrm(ks[16], (L, D_CONV), 0.02),
        "rpb": nrm(ks[17], (L, N_HEADS_NA, 2 * WIN_R - 1, 2 * WIN_C - 1), 0.1),
        "w_gate": nrm(ks[18], (L, D_MODEL, N_BRANCH * D_MODEL), D_MODEL ** -0.5),
        "b_gate": nrm(ks[19], (L, N_BRANCH * D_MODEL), 0.02),
        "w_b_na": nrm(ks[20], (L, D_NA, D_MODEL), BETA * D_NA ** -0.5),
        "w_b_f": nrm(ks[21], (L, D_FOURIER, D_MODEL), BETA * D_FOURIER ** -0.5),
        "w_b_c": nrm(ks[22], (L, D_CONV, D_MODEL), BETA * D_CONV ** -0.5),
        "w_o": nrm(ks[23], (L, D_MODEL, D_MODEL), BETA * D_MODEL ** -0.5),
        "ln_g": 1.0 + nrm(ks[24], (L, 3, D_MODEL), 0.02),
        "ln_b": nrm(ks[25], (L, 3, D_MODEL), 0.02),
    }


def reference(x, c, ctx, c_ctx, w_ada, b_ada, ffn1_w1, ffn1_w3, ffn1_w2, ffn2_w1, ffn2_w3, ffn2_w2,
              w_in, w_dw, b_dw, conv_ln_g, conv_ln_b, rpb, w_gate, b_gate, w_b_na, w_b_f, w_b_c,
              w_o, ln_g, ln_b):
    b_ = x.shape[0]
    xc = ctx
    for l in range(DEPTH):
        last = l == DEPTH - 1
        mod = (jax.nn.silu(c) @ w_ada[l] + b_ada[l]).reshape(b_, 1, N_MOD, D_MODEL)
        modc = (jax.nn.silu(c_ctx) @ w_ada[l] + b_ada[l]).reshape(1, 1, N_MOD, D_MODEL)

        x = ffn_sublayer(x, mod, 0, ffn1_w1[l], ffn1_w3[l], ffn1_w2[l], ln_g[l, 0], ln_b[l, 0])
        xc = ffn_sublayer(xc, modc, 0, ffn1_w1[l], ffn1_w3[l], ffn1_w2[l], ln_g[l, 0], ln_b[l, 0])

        h = modulate(x, mod[:, :, 3], mod[:, :, 4])
        hc = modulate(xc, modc[:, :, 3], modc[:, :, 4])
        q, k, v, f, a, g = split_in(h @ w_in[l])
        if last:
            kv_c = (hc @ w_in[l][:, D_NA:3 * D_NA]).reshape(b_, -1, 2, N_HEADS_NA, HEAD_DIM)
            kc, vc = kv_c[:, :, 0], kv_c[:, :, 1]
        else:
            qc, kc, vc, fc, ac, gc = split_in(hc @ w_in[l])

        y_na = neighbourhood_attention(q, k, v, kc, vc, rpb[l])
        y_f = fourier_mix(f)
        y_c = conv_module(a, g, w_dw[l], b_dw[l], conv_ln_g[l], conv_ln_b[l])
        out = merge_branches(h, y_na, y_f, y_c, w_gate[l], b_gate[l], w_b_na[l], w_b_f[l], w_b_c[l], w_o[l])
        x = layer_norm(ALPHA * x + mod[:, :, 5] * out, ln_g[l, 1], ln_b[l, 1])

        if not last:
            yc_na = context_attention(qc, kc, vc)
            yc_f = fourier_mix(fc)
            yc_c = conv_module(ac, gc, w_dw[l], b_dw[l], conv_ln_g[l], conv_ln_b[l])
            outc = merge_branches(hc, yc_na, yc_f, yc_c, w_gate[l], b_gate[l], w_b_na[l], w_b_f[l], w_b_c[l], w_o[l])
            xc = layer_norm(ALPHA * xc + modc[:, :, 5] * outc, ln_g[l, 1], ln_b[l, 1])

        x = ffn_sublayer(x, mod, 1, ffn2_w1[l], ffn2_w3[l], ffn2_w2[l], ln_g[l, 2], ln_b[l, 2])
        if not last:
            xc = ffn_sublayer(xc, modc, 1, ffn2_w1[l], ffn2_w3[l], ffn2_w2[l], ln_g[l, 2], ln_b[l, 2])
    return x
```

```python
from contextlib import ExitStack
import numpy as np
import ml_dtypes
import concourse.bass as bass
import concourse.mybir as mybir
from concourse.bass_utils import run_bass_kernel_spmd

F32 = mybir.dt.float32
BF16 = mybir.dt.bfloat16
AF = mybir.ActivationFunctionType
ALU = mybir.AluOpType
NPBF = ml_dtypes.bfloat16

D = 2048
KC = 16
SEQ = 8192
DEPTH = 2
NCORE = 8
TL = 1024
TCX = 32
NT = TL + TCX
CTX = 256
DFF = 5632
NJ = 44
DNA = 1024
DIN = 4608
GRID_W = 64
ALPHA = (2 * DEPTH) ** 0.25
LN_EPS = 1e-5
EPS_S = LN_EPS / (ALPHA * ALPHA)
NEG = -1e30
WQ = "sp"
SKIP = set()


class R:
    __slots__ = ("w", "rd")

    def __init__(self):
        self.w = None
        self.rd = {}


class T:
    def __init__(self, t, nres=1):
        self.t = t
        self.rs = [R() for _ in range(nres)]

    @property
    def r(self):
        return self.rs[0]


class KB:
    NDMA = 20

    def __init__(self, nc, es):
        self.nc = nc
        self.es = es
        self.eng = {"pe": nc.tensor, "act": nc.scalar, "dve": nc.vector, "pool": nc.gpsimd, "sp": nc.sync}
        self.psem = {e: es.enter_context(nc.semaphore("p_" + e)) for e in ("pe", "act", "dve", "pool")}
        self.pcnt = {e: 0 for e in self.psem}
        self.seen = {e: {} for e in self.eng}
        self.dsem = [es.enter_context(nc.semaphore("d%d" % i)) for i in range(self.NDMA)]
        self.dcnt = [0] * self.NDMA
        self.rr = 0

    uid = 0

    def sb(self, name, shape, dt, nres=1):
        KB.uid += 1
        return T(self.es.enter_context(self.nc.sbuf_tensor("%s_%d" % (name, KB.uid), list(shape), dt)), nres)

    def ps(self, name, shape, dt=F32, nres=1):
        KB.uid += 1
        return T(self.es.enter_context(self.nc.psum_tensor("%s_%d" % (name, KB.uid), list(shape), dt)), nres)

    def _waits(self, e, rd, wr, extra=None):
        need = dict(extra or {})
        for r in rd:
            if r.w is not None and need.get(r.w[0], 0) < r.w[1]:
                need[r.w[0]] = r.w[1]
        for r in wr:
            if r.w is not None and need.get(r.w[0], 0) < r.w[1]:
                need[r.w[0]] = r.w[1]
            for s, v in r.rd.items():
                if need.get(s, 0) < v:
                    need[s] = v
        eng = self.eng[e]
        seen = self.seen[e]
        own = self.psem.get(e)
        for s, v in need.items():
            if e == "pe" and s is own:
                continue
            if seen.get(s, 0) < v:
                eng.wait_ge(s, v)
                seen[s] = v

    def _mark(self, tok, rd, wr):
        s, v = tok
        for r in rd:
            if r.rd.get(s, 0) < v:
                r.rd[s] = v
        for r in wr:
            r.w = tok
            r.rd = {}

    def op(self, e, fn, rd=(), wr=()):
        self._waits(e, rd, wr)
        ins = fn(self.eng[e])
        self.pcnt[e] += 1
        ins.then_inc(self.psem[e], 1)
        self._mark((self.psem[e], self.pcnt[e]), rd, wr)

    def dma(self, e, out, in_, rd=(), wr=()):
        k = self.rr
        self.rr = (self.rr + 1) % self.NDMA
        extra = {self.dsem[k]: self.dcnt[k]} if self.dcnt[k] else None
        self._waits(e, rd, wr, extra)
        ins = self.eng[e].dma_start(out=out, in_=in_)
        self.dcnt[k] += 16
        ins.then_inc(self.dsem[k], 16)
        self._mark((self.dsem[k], self.dcnt[k]), rd, wr)

    def finish(self):
        for k in range(self.NDMA):
            if self.dcnt[k]:
                self.nc.sync.wait_ge(self.dsem[k], self.dcnt[k])
        for e, s in self.psem.items():
            if self.pcnt[e]:
                self.nc.sync.wait_ge(s, self.pcnt[e])


class Ctx:
    pass


def setup_common(kb, nblk_cols=512):
    c = Ctx()
    c.P = [kb.ps("P%d" % i, [128, 512]) for i in range(6)]
    c.ones = kb.sb("ones", [128, 128], F32)
    kb.op("dve", lambda e: e.memset(c.ones.t[:], 1.0 / D), wr=[c.ones.r])
    return c


def alloc_p67(kb, c):
    c.P = c.P[:6] + [kb.ps("P6", [128, 512]), kb.ps("P7", [128, 512])]


def alloc_pt(kb, c):
    c.PT = [kb.ps("PT%d" % i, [128, 8, 128], BF16) for i in range(2)]


def load_mod(kb, c, mod_ap, lnp_ap, layer):
    c.mod = kb.sb("mod%d" % layer, [128, 9, KC, 2], F32)
    c.lnp = kb.sb("lnp%d" % layer, [128, 6, KC], F32)
    c.sc1 = kb.sb("sc1_%d" % layer, [128, 3, KC, 2], F32)
    c.gsc = kb.sb("gsc_%d" % layer, [128, 3, KC, 2], F32)
    kb.dma("sp", c.mod.t[:].rearrange("p a k w -> p (a k w)"), mod_ap[layer], wr=[c.mod.r])
    kb.dma("sp", c.lnp.t[:].rearrange("p a k -> p (a k)"), lnp_ap[layer], wr=[c.lnp.r])
    for s in range(3):
        ms = min(s, 1)
        kb.op("dve", lambda e, s=s: e.tensor_scalar(out=c.sc1.t[:, s], in0=c.mod.t[:, 3 * ms + 1], scalar1=1.0,
                                                     scalar2=None, op0=ALU.add), rd=[c.mod.r], wr=[c.sc1.r])
        f = (0.5 if s != 1 else 1.0) / ALPHA
        kb.op("dve", lambda e, s=s, f=f: e.tensor_scalar(out=c.gsc.t[:, s], in0=c.mod.t[:, 3 * ms + 2], scalar1=f,
                                                          scalar2=None, op0=ALU.mult), rd=[c.mod.r], wr=[c.gsc.r])


def blk_w(c0):
    return 1 if c0 >= TL else 0


def emit_modulate(kb, c, x, h, sub, blks, hoff=0):
    i = 0
    for kc in range(KC):
        for (c0, n) in blks:
            w = blk_w(c0)
            i += 1
            if i % 2:
                kb.op("dve", lambda e, kc=kc, c0=c0, n=n, w=w: e.tensor_scalar(
                    out=h.t[:, kc, c0 - hoff:c0 - hoff + n], in0=x.t[:, kc, c0:c0 + n],
                    scalar1=c.sc1.t[:, sub, kc, w:w + 1], scalar2=c.mod.t[:, 3 * sub, kc, w:w + 1],
                    op0=ALU.mult, op1=ALU.add),
                    rd=[x.rs[kc], c.sc1.r, c.mod.r], wr=[h.rs[kc]])
            else:
                kb.op("act", lambda e, kc=kc, c0=c0, n=n, w=w: e.activation(
                    out=h.t[:, kc, c0 - hoff:c0 - hoff + n], in_=x.t[:, kc, c0:c0 + n], func=AF.Identity,
                    scale=c.sc1.t[:, sub, kc, w:w + 1], bias=c.mod.t[:, 3 * sub, kc, w:w + 1]),
                    rd=[x.rs[kc], c.sc1.r, c.mod.r], wr=[h.rs[kc]])


def emit_ln(kb, c, x, sub, blks, tmp):
    emit_ln_g(kb, c, x, KC, blks, tmp, c.ones, EPS_S,
              lambda kc: c.lnp.t[:, 2 * sub, kc:kc + 1], lambda kc: c.lnp.t[:, 2 * sub + 1, kc:kc + 1], [c.lnp.r])


def emit_ln_g(kb, c, x, nch, blks, tmp, ones, eps, gfn, bfn, prs, out=None, func=None):
    mean_ps, ex2_ps = c.P[6], c.P[7]
    func = func or AF.Identity
    for (c0, n) in blks:
        mean, var, rstd, nmr = tmp["mean"], tmp["var"], tmp["rstd"], tmp["nmr"]
        for kc in range(nch):
            if kc == 0:
                kb.op("act", lambda e: e.activation(out=var.t[:, 0:n], in_=x.t[:, kc, c0:c0 + n], func=AF.Square),
                      rd=[x.rs[kc]], wr=[var.r])
                continue
            sq = tmp["sq"][kc % 2]
            kb.op("act", lambda e: e.activation(out=sq.t[:, 0:n], in_=x.t[:, kc, c0:c0 + n], func=AF.Square),
                  rd=[x.rs[kc]], wr=[sq.r])
            kb.op("dve", lambda e: e.tensor_tensor(out=var.t[:, 0:n], in0=var.t[:, 0:n], in1=sq.t[:, 0:n], op=ALU.add),
                  rd=[var.r, sq.r], wr=[var.r])
            if kc == 1:
                kb.op("pool", lambda e: e.tensor_tensor(out=mean.t[:, 0:n], in0=x.t[:, 0, c0:c0 + n], in1=x.t[:, 1, c0:c0 + n], op=ALU.add),
                      rd=[x.rs[0], x.rs[1]], wr=[mean.r])
            else:
                kb.op("pool", lambda e: e.tensor_tensor(out=mean.t[:, 0:n], in0=mean.t[:, 0:n], in1=x.t[:, kc, c0:c0 + n], op=ALU.add),
                      rd=[mean.r, x.rs[kc]], wr=[mean.r])
        kb.op("pe", lambda e: e.matmul(mean_ps.t[:, 0:n], ones.t[:], mean.t[:, 0:n], start=True, stop=True),
              rd=[mean.r, ones.r], wr=[mean_ps.r])
        kb.op("pe", lambda e: e.matmul(ex2_ps.t[:, 0:n], ones.t[:], var.t[:, 0:n], start=True, stop=True),
              rd=[var.r, ones.r], wr=[ex2_ps.r])
        kb.op("act", lambda e: e.copy(out=mean.t[:, 0:n], in_=mean_ps.t[:, 0:n]), rd=[mean_ps.r], wr=[mean.r])
        kb.op("dve", lambda e: e.tensor_tensor(out=var.t[:, 0:n], in0=mean.t[:, 0:n], in1=mean.t[:, 0:n], op=ALU.mult),
              rd=[mean.r], wr=[var.r])
        kb.op("dve", lambda e: e.tensor_tensor(out=var.t[:, 0:n], in0=ex2_ps.t[:, 0:n], in1=var.t[:, 0:n], op=ALU.subtract),
              rd=[ex2_ps.r, var.r], wr=[var.r])
        kb.op("act", lambda e: e.activation(out=var.t[:, 0:n], in_=var.t[:, 0:n], func=AF.Sqrt, bias=eps, scale=1.0),
              rd=[var.r], wr=[var.r])
        kb.op("dve", lambda e: e.reciprocal(out=rstd.t[:, 0:n], in_=var.t[:, 0:n]), rd=[var.r], wr=[rstd.r])
        kb.op("dve", lambda e: e.scalar_tensor_tensor(out=nmr.t[:, 0:n], in0=mean.t[:, 0:n], scalar=-1.0, in1=rstd.t[:, 0:n],
                                                      op0=ALU.mult, op1=ALU.mult), rd=[mean.r, rstd.r], wr=[nmr.r])
        for kc in range(nch):
            t1 = tmp["t1"][kc % 2]
            kb.op("dve", lambda e: e.tensor_tensor(out=t1.t[:, 0:n], in0=x.t[:, kc, c0:c0 + n], in1=rstd.t[:, 0:n],
                                                   op=ALU.mult), rd=[x.rs[kc], rstd.r], wr=[t1.r])
            kb.op("dve", lambda e: e.tensor_tensor(out=t1.t[:, 0:n], in0=t1.t[:, 0:n], in1=nmr.t[:, 0:n], op=ALU.add),
                  rd=[t1.r, nmr.r], wr=[t1.r])
            o = x if out is None else out
            kb.op("act", lambda e: e.activation(out=o.t[:, kc, c0:c0 + n], in_=t1.t[:, 0:n], func=func,
                                                scale=gfn(kc), bias=bfn(kc)),
                  rd=[t1.r] + prs, wr=[o.rs[kc]])


def make_ln_tmp(kb):
    tmp = {}
    tmp["sq"] = [kb.sb("ln_sq%d" % i, [128, 512], F32) for i in range(2)]
    tmp["t1"] = [kb.sb("ln_t1%d" % i, [128, 512], F32) for i in range(2)]
    for nm in ("mean", "var", "rstd", "nmr"):
        tmp[nm] = kb.sb("ln_" + nm, [128, 512], F32)
    return tmp


def emit_ffn(kb, c, x, sub, w13_ap, w2_ap, passes, bufs):
    h, g, w13, w2t, sa = bufs["h"], bufs["g"], bufs["w13"], bufs["w2"], bufs["sa"]
    P = c.P
    for blks in passes:
        hoff = blks[0][0]
        def hc(c0):
            return (c0 - hoff) if c0 < TL else 512
        i = 0
        for kc in range(KC):
            for (c0, n) in blks:
                w = blk_w(c0)
                i += 1
                if i % 2:
                    kb.op("dve", lambda e, kc=kc, c0=c0, n=n, w=w: e.tensor_scalar(
                        out=h.t[:, kc, hc(c0):hc(c0) + n], in0=x.t[:, kc, c0:c0 + n],
                        scalar1=c.sc1.t[:, sub, kc, w:w + 1], scalar2=c.mod.t[:, 3 * min(sub, 1), kc, w:w + 1],
                        op0=ALU.mult, op1=ALU.add), rd=[x.rs[kc], c.sc1.r, c.mod.r], wr=[h.rs[kc]])
                else:
                    kb.op("act", lambda e, kc=kc, c0=c0, n=n, w=w: e.activation(
                        out=h.t[:, kc, hc(c0):hc(c0) + n], in_=x.t[:, kc, c0:c0 + n], func=AF.Identity,
                        scale=c.sc1.t[:, sub, kc, w:w + 1], bias=c.mod.t[:, 3 * min(sub, 1), kc, w:w + 1]),
                        rd=[x.rs[kc], c.sc1.r, c.mod.r], wr=[h.rs[kc]])
        def load13(jj):
            for s in range(2):
                kb.dma(WQ, w13[s][jj % 2].t[:], w13_ap[s, jj], wr=[w13[s][jj % 2].r])
        load13(0)
        for jj in range(22):
            if jj + 1 < 22:
                load13(jj + 1)
            for jl in range(2):
                j = 2 * jj + jl
                pb = (j % 2) * 3
                for s in range(2):
                    wt = w13[s][jj % 2]
                    for kc in range(KC):
                        for (c0, n) in blks:
                            if c0 < TL:
                                o = P[pb + s].t[:, 0:n]
                                ro = P[pb + s].r
                            else:
                                o = P[pb + 2].t[:, 32 * s:32 * s + n]
                                ro = P[pb + 2].rs[0]
                            kb.op("pe", lambda e, o=o, wt=wt, kc=kc, c0=c0, n=n: e.matmul(
                                o, wt.t[:, kc, 128 * jl:128 * jl + 128], h.t[:, kc, hc(c0):hc(c0) + n],
                                start=(kc == 0), stop=(kc == KC - 1)), rd=[wt.r, h.rs[kc]], wr=[ro])
                for (c0, n) in blks:
                    st = sa[j % 2]
                    if c0 < TL:
                        a_ap, b_ap, ra = P[pb].t[:, 0:n], P[pb + 1].t[:, 0:n], [P[pb].r, P[pb + 1].r]
                        so = st.t[:, 0:n]
                    else:
                        a_ap, b_ap, ra = P[pb + 2].t[:, 0:n], P[pb + 2].t[:, 32:32 + n], [P[pb + 2].r]
                        so = st.t[:, 512:512 + n]
                    kb.op("act", lambda e, so=so, a_ap=a_ap: e.activation(out=so, in_=a_ap, func=AF.Silu), rd=ra, wr=[st.r])
                    kb.op("dve", lambda e, so=so, b_ap=b_ap, c0=c0, n=n, j=j: e.tensor_tensor(
                        out=g.t[:, j, hc(c0):hc(c0) + n], in0=so, in1=b_ap, op=ALU.mult), rd=ra + [st.r], wr=[g.rs[j]])
        def load2(q):
            kb.dma(WQ, w2t[q % 2].t[:], w2_ap[q // 4, q % 4], wr=[w2t[q % 2].r])
        load2(0)
        for dg in range(8):
            pb = (dg % 2) * 3
            for jb in range(4):
                q = dg * 4 + jb
                if q + 1 < 32:
                    load2(q + 1)
                wt = w2t[q % 2]
                for ji in range(11):
                    j = jb * 11 + ji
                    for dl in range(2):
                        for (c0, n) in blks:
                            if c0 < TL:
                                o, ro = P[pb + dl].t[:, 0:n], P[pb + dl].r
                            else:
                                cb = P[pb + 2] if dl == 0 else P[6 + dg % 2]
                                o, ro = cb.t[:, 0:n], cb.r
                            kb.op("pe", lambda e, o=o, wt=wt, ji=ji, dl=dl, j=j, c0=c0, n=n: e.matmul(
                                o, wt.t[:, ji, 128 * dl:128 * dl + 128], g.t[:, j, hc(c0):hc(c0) + n],
                                start=(j == 0), stop=(j == NJ - 1)), rd=[wt.r, g.rs[j]], wr=[ro])
            for dl in range(2):
                d = 2 * dg + dl
                for (c0, n) in blks:
                    w = blk_w(c0)
                    if c0 < TL:
                        y_ap, ry = P[pb + dl].t[:, 0:n], P[pb + dl].r
                    else:
                        cb = P[pb + 2] if dl == 0 else P[6 + dg % 2]
                        y_ap, ry = cb.t[:, 0:n], cb.r
                    kb.op("dve", lambda e, y_ap=y_ap, d=d, c0=c0, n=n, w=w: e.scalar_tensor_tensor(
                        out=x.t[:, d, c0:c0 + n], in0=y_ap, scalar=c.gsc.t[:, sub, d, w:w + 1], in1=x.t[:, d, c0:c0 + n],
                        op0=ALU.mult, op1=ALU.add), rd=[ry, c.gsc.r, x.rs[d]], wr=[x.rs[d]])
        emit_ln(kb, c, x, sub, blks, bufs["ln"])


def make_ffn_bufs(kb):
    b = {}
    b["h"] = kb.sb("ffn_h", [128, KC, 544], BF16, nres=KC)
    b["g"] = kb.sb("ffn_g", [128, NJ, 544], BF16, nres=NJ)
    b["w13"] = [[kb.sb("w13_%d_%d" % (s, i), [128, KC, 256], BF16) for i in range(2)] for s in range(2)]
    b["w2"] = [kb.sb("w2_%d" % i, [128, 11, 256], BF16) for i in range(2)]
    b["sa"] = [kb.sb("sa%d" % i, [128, 544], F32) for i in range(2)]
    b["ln"] = make_ln_tmp(kb)
    return b


FFN_PASSES = [[(0, 512), (TL, TCX)], [(512, 512)]]


def build_mod():
    nc = bass.Bass("TRN2", target_bir_lowering=False)
    NCOL = 2304
    HC = NCOL // 2
    wada = nc.dram_tensor("wada", [DEPTH, 2, 128, KC, HC], F32, kind="ExternalInput").ap()
    bada = nc.dram_tensor("bada", [2, DEPTH, NCOL], F32, kind="ExternalInput").ap()
    cc = nc.dram_tensor("cc", [128, KC, 2], F32, kind="ExternalInput").ap()
    out = nc.dram_tensor("modo", [2, DEPTH, NCOL], F32, kind="ExternalOutput").ap()
    wci = nc.dram_tensor("wci", [128, WC_M], F32, kind="ExternalInput").ap()
    wco = nc.dram_tensor("wco", [128, WC_M], BF16, kind="ExternalOutput").ap()
    with ExitStack() as es:
        kb = KB(nc, es)
        with ExitStack() as es2:
            kb.es = es2
            cb = [kb.sb("cb%d" % i, [128, WC_CH], BF16) for i in range(4)]
            for q in range(WC_M // WC_CH):
                t_ = cb[q % 4]
                kb.dma("pool", t_.t[:], wci[:, q * WC_CH:(q + 1) * WC_CH], wr=[t_.r])
                kb.dma("sp", wco[:, q * WC_CH:(q + 1) * WC_CH], t_.t[:], rd=[t_.r])
            kb.barrier()
            kb.es = es
        ct = kb.sb("ct", [128, KC, 2], F32)
        sg = kb.sb("sg", [128, KC, 2], F32)
        wt = [kb.sb("wt%d" % i, [128, KC, HC], F32) for i in range(2)]
        bt = kb.sb("bt", [2, DEPTH, NCOL], F32)
        ot = kb.sb("ot", [2, DEPTH, NCOL], F32)
        ps = [kb.ps("ps%d" % i, [128, 512]) for i in range(6)]
        kb.dma("sp", ct.t[:], cc, wr=[ct.r])
        kb.dma("sp", bt.t[:], bada, wr=[bt.r])
        kb.op("act", lambda e: e.activation(out=sg.t[:], in_=ct.t[:], func=AF.Silu), rd=[ct.r], wr=[sg.r])
        q = 0
        for l in range(DEPTH):
            for hh in range(2):
                w_ = wt[q % 2]
                kb.dma("sp", w_.t[:], wada[l, hh], wr=[w_.r])
                for bi, (n0, n) in enumerate([(0, 512), (512, 512), (1024, HC - 1024)]):
                    pt = ps[(q % 2) * 3 + bi]
                    for kc in range(KC):
                        kb.op("pe", lambda e: e.matmul(pt.t[0:2, 0:n], sg.t[:, kc, :], w_.t[:, kc, n0:n0 + n],
                                                       start=(kc == 0), stop=(kc == KC - 1)), rd=[w_.r, sg.r], wr=[pt.r])
                    c0 = hh * HC + n0
                    kb.op("dve", lambda e: e.tensor_tensor(out=ot.t[:, l, c0:c0 + n], in0=pt.t[0:2, 0:n], in1=bt.t[:, l, c0:c0 + n],
                                                           op=ALU.add), rd=[pt.r, bt.r], wr=[ot.r])
                q += 1
        kb.dma("sp", out, ot.t[:], rd=[ot.r])
        kb.finish()
    return nc


WC_NAMES = [("f1w13", (2, 22, 128, KC, 256)), ("f1w2", (8, 4, 128, 11, 256)), ("f2w13", (2, 22, 128, KC, 256)),
            ("f2w2", (8, 4, 128, 11, 256)), ("winf", (14, 128, KC, 256)), ("winv", (2, 128, KC, 512)),
            ("wgb", (16, 128, 64, 128)), ("wo", (16, 128, KC, 128))]
WC_SIZES = [int(np.prod(sh)) // (NCORE * 128) for _, sh in WC_NAMES]
WC_M = DEPTH * sum(WC_SIZES)
WC_CH = 4864
assert WC_M % WC_CH == 0


def tiled_weights(inp, l):
    winf, winv = tile_win(inp["w_in"][l])
    tg = inp["w_gate"][l].reshape(KC, 128, 3, 16, 128).transpose(3, 1, 2, 0, 4).reshape(16, 128, 48, 128)
    tbn = inp["w_b_na"][l].reshape(8, 128, 16, 128).transpose(2, 1, 0, 3)
    tbf = inp["w_b_f"][l].reshape(4, 128, 16, 128).transpose(2, 1, 0, 3)
    tbc = inp["w_b_c"][l].reshape(4, 128, 16, 128).transpose(2, 1, 0, 3)
    return {"f1w13": tile_w13(inp["ffn1_w1"][l], inp["ffn1_w3"][l]), "f1w2": tile_w2(inp["ffn1_w2"][l]),
            "f2w13": tile_w13(inp["ffn2_w1"][l], inp["ffn2_w3"][l]), "f2w2": tile_w2(inp["ffn2_w2"][l]),
            "winf": winf, "winv": winv, "wgb": np.ascontiguousarray(np.concatenate([tg, tbn, tbf, tbc], axis=2)),
            "wo": np.ascontiguousarray(inp["w_o"][l].reshape(KC, 128, 16, 128).transpose(2, 1, 0, 3))}


def run_mod(c, c_ctx, w_ada, b_ada, inp):
    tw = [tiled_weights(inp, l) for l in range(DEPTH)]
    slabs = [[] for _ in range(NCORE)]
    for l in range(DEPTH):
        for (nm, sh), m in zip(WC_NAMES, WC_SIZES):
            a = tw[l][nm].reshape(NCORE, 128, m)
            for i in range(NCORE):
                slabs[i].append(a[i])
    del tw
    cc = np.stack([c.reshape(KC, 128).T, c_ctx.reshape(KC, 128).T], axis=-1)
    cc = np.ascontiguousarray(cc, dtype=np.float32)
    in_maps = []
    for i in range(NCORE):
        ws = w_ada[:, :, 2304 * i:2304 * (i + 1)]
        ws = np.ascontiguousarray(ws.reshape(DEPTH, KC, 128, 2, 1152).transpose(0, 3, 2, 1, 4))
        bs = np.ascontiguousarray(np.broadcast_to(b_ada[None, :, 2304 * i:2304 * (i + 1)], (2, DEPTH, 2304)))
        in_maps.append({"wada": ws, "bada": bs, "cc": cc, "wci": np.ascontiguousarray(np.concatenate(slabs[i], axis=1))})
    del slabs
    res = run_bass_kernel_spmd(build_mod(), in_maps, core_ids=list(range(NCORE)))
    del in_maps
    wco = [np.asarray(res.results[i]["wco"]) for i in range(NCORE)]
    wb = []
    off = 0
    for l in range(DEPTH):
        d = {}
        for (nm, sh), m in zip(WC_NAMES, WC_SIZES):
            d[nm] = np.ascontiguousarray(np.stack([wco[i][:, off:off + m] for i in range(NCORE)], axis=0)).reshape(sh)
            off += m
        wb.append(d)
    mod = np.zeros((DEPTH, 2, 9 * D), np.float32)
    for i in range(NCORE):
        o = np.asarray(res.results[i]["modo"])
        mod[:, :, 2304 * i:2304 * (i + 1)] = o.transpose(1, 0, 2)
    modT = mod.reshape(DEPTH, 2, 9, KC, 128).transpose(0, 4, 2, 3, 1).reshape(DEPTH, 128, 9 * KC * 2)
    return np.ascontiguousarray(modT), wb


TOK_TILES = [(128 * i, 128) for i in range(8)] + [(TL, TCX)]
ALL_BLKS = [(0, 512), (512, 512), (TL, TCX)]


def emit_proj(kb, c, x, winf_ap, winv_ap, cs128_ap, outs, kv_only=False):
    P = c.P
    with ExitStack() as es2:
        old = kb.es
        kb.es = es2
        h2 = kb.sb("pj_h2", [128, KC, NT], BF16, nres=KC)
        wf = [kb.sb("pj_wf%d" % i, [128, KC, 256], BF16) for i in range(2)]
        wv = [kb.sb("pj_wv%d" % i, [128, KC, 512], BF16) for i in range(2)]
        stg = [kb.sb("pj_stg%d" % i, [128, NT], BF16) for i in range(2)]
        ustg = [kb.sb("pj_us%d" % i, [128, NT], F32) for i in range(2)]
        sgt = [kb.sb("pj_sg%d" % i, [128, NT], F32) for i in range(2)]
        fT = kb.sb("pj_fT", [128, 4, NT], BF16, nres=4)
        cs = kb.sb("pj_cs", [128, 256], BF16)
        vst = [kb.sb("pj_vst%d" % i, [128, 1024], BF16) for i in range(2)]
        kb.dma("pool", cs.t[:], cs128_ap, wr=[cs.r])
        emit_modulate(kb, c, x, h2, 1, ALL_BLKS)
        for hh in range(2):
            kb.dma("pool", wv[hh].t[:], winv_ap[hh], wr=[wv[hh].r])
        for ti, (t0, tn) in enumerate(TOK_TILES):
            if kv_only and t0 < TL:
                continue
            vs = vst[ti % 2]
            for hh in range(2):
                pt = P[(ti % 2) * 2 + hh]
                for kc in range(KC):
                    kb.op("pe", lambda e, kc=kc, pt=pt, hh=hh: e.matmul(pt.t[0:tn, :], h2.t[:, kc, t0:t0 + tn], wv[hh].t[:, kc, :],
                                                                       start=(kc == 0), stop=(kc == KC - 1)),
                          rd=[h2.rs[kc], wv[hh].r], wr=[pt.r])
                eng = ("act", "dve")[hh]
                if eng == "act":
                    kb.op("act", lambda e, pt=pt, hh=hh: e.copy(out=vs.t[0:tn, 512 * hh:512 * hh + 512], in_=pt.t[0:tn, :]),
                          rd=[pt.r], wr=[vs.r])
                else:
                    kb.op("dve", lambda e, pt=pt, hh=hh: e.tensor_copy(out=vs.t[0:tn, 512 * hh:512 * hh + 512], in_=pt.t[0:tn, :]),
                          rd=[pt.r], wr=[vs.r])
            kb.dma("sp", outs["V"][t0:t0 + tn, :], vs.t[0:tn, :], rd=[vs.r])
        blks = [(TL, TCX)] if kv_only else ALL_BLKS
        pairs = list(range(4, 8)) if kv_only else list(range(14))
        kb.dma("pool", wf[pairs[0] % 2].t[:], winf_ap[pairs[0]], wr=[wf[pairs[0] % 2].r])
        for pi, pr in enumerate(pairs):
            if pi + 1 < len(pairs):
                nx = pairs[pi + 1]
                kb.dma("pool", wf[nx % 2].t[:], winf_ap[nx], wr=[wf[nx % 2].r])
            wt = wf[pr % 2]
            for cl in range(2):
                pb = 3 * cl
                for kc in range(KC):
                    for (c0, n) in blks:
                        pt = P[pb + (0 if c0 == 0 else 1 if c0 == 512 else 2)]
                        kb.op("pe", lambda e, pt=pt, kc=kc, c0=c0, n=n, cl=cl: e.matmul(
                            pt.t[:, 0:n], wt.t[:, kc, 128 * cl:128 * cl + 128], h2.t[:, kc, c0:c0 + n],
                            start=(kc == 0), stop=(kc == KC - 1)), rd=[wt.r, h2.rs[kc]], wr=[pt.r])
                if pr < 10:
                    ch = 2 * pr + cl
                    st = stg[ch % 2]
                    for bi, (c0, n) in enumerate(blks):
                        pt = P[pb + (0 if c0 == 0 else 1 if c0 == 512 else 2)]
                        dst = fT.t[:, ch - 16, c0:c0 + n] if ch >= 16 else st.t[:, c0:c0 + n]
                        wr = [fT.rs[ch - 16]] if ch >= 16 else [st.r]
                        if bi % 2 == 0:
                            kb.op("act", lambda e, pt=pt, dst=dst, n=n: e.copy(out=dst, in_=pt.t[:, 0:n]), rd=[pt.r], wr=wr)
                        else:
                            kb.op("dve", lambda e, pt=pt, dst=dst, n=n: e.tensor_copy(out=dst, in_=pt.t[:, 0:n]), rd=[pt.r], wr=wr)
                    if ch < 16:
                        dram = outs["qT"] if ch < 8 else outs["kT"]
                        if kv_only:
                            kb.dma("sp", dram[:, ch % 8, TL:NT], st.t[:, TL:NT], rd=[st.r])
                        else:
                            kb.dma("sp", dram[:, ch % 8, :], st.t[:, :], rd=[st.r])
            if pr >= 10:
                i = pr - 10
                sg, us = sgt[i % 2], ustg[i % 2]
                for (c0, n) in blks:
                    k = 0 if c0 == 0 else 1 if c0 == 512 else 2
                    kb.op("act", lambda e, k=k, c0=c0, n=n: e.activation(out=sg.t[:, c0:c0 + n], in_=P[3 + k].t[:, 0:n], func=AF.Sigmoid),
                          rd=[P[3 + k].r], wr=[sg.r])
                    kb.op("dve", lambda e, k=k, c0=c0, n=n: e.tensor_tensor(out=us.t[:, c0:c0 + n], in0=sg.t[:, c0:c0 + n],
                                                                            in1=P[k].t[:, 0:n], op=ALU.mult),
                          rd=[P[k].r, P[3 + k].r, sg.r], wr=[us.r])
                kb.dma("sp", outs["uT"][:, i, :], us.t[:, :], rd=[us.r])
        if not kv_only:
            for ti, (t0, tn) in enumerate(TOK_TILES):
                vs = vst[ti % 2]
                for gi in range(4):
                    for w in range(2):
                        kb.op("pe", lambda e, gi=gi, w=w: e.matmul(P[6 + w].t[0:tn, 128 * gi:128 * gi + 128], fT.t[:, gi, t0:t0 + tn],
                                                                   cs.t[:, 128 * w:128 * w + 128], start=True, stop=True),
                              rd=[fT.rs[gi], cs.r], wr=[P[6 + w].r])
                kb.op("act", lambda e: e.copy(out=vs.t[0:tn, 0:512], in_=P[6].t[0:tn, :]), rd=[P[6].r], wr=[vs.r])
                kb.op("dve", lambda e: e.tensor_copy(out=vs.t[0:tn, 512:1024], in_=P[7].t[0:tn, :]), rd=[P[7].r], wr=[vs.r])
                kb.dma("sp", outs["XcXs"][t0:t0 + tn, :], vs.t[0:tn, :], rd=[vs.r])
        kb.barrier()
        kb.es = old


def _barrier(self):
    for e, eng in self.eng.items():
        seen = self.seen[e]
        for e2, s in self.psem.items():
            if e2 != e and self.pcnt[e2] and seen.get(s, 0) < self.pcnt[e2]:
                eng.wait_ge(s, self.pcnt[e2])
                seen[s] = self.pcnt[e2]
        for k in range(self.NDMA):
            if self.dcnt[k] and seen.get(self.dsem[k], 0) < self.dcnt[k]:
                eng.wait_ge(self.dsem[k], self.dcnt[k])
                seen[self.dsem[k]] = self.dcnt[k]


KB.barrier = _barrier


def dram_in(nc, name, shape, dt=F32):
    return nc.dram_tensor(name, list(shape), dt, kind="ExternalInput").ap()


def dram_out(nc, name, shape, dt=F32):
    return nc.dram_tensor(name, list(shape), dt, kind="ExternalOutput").ap()


def decl_proj_outs(nc):
    return {"qT": dram_out(nc, "o_qT", [128, 8, NT], BF16), "kT": dram_out(nc, "o_kT", [128, 8, NT], BF16),
            "V": dram_out(nc, "o_V", [NT, 1024], BF16), "XcXs": dram_out(nc, "o_XcXs", [NT, 1024], BF16),
            "uT": dram_out(nc, "o_uT", [128, 4, NT], F32)}


def build_stage1():
    nc = bass.Bass("TRN2", target_bir_lowering=False)
    xT = dram_in(nc, "xT", [128, KC, NT])
    mod = dram_in(nc, "mod", [DEPTH, 128, 9 * KC * 2])
    lnp = dram_in(nc, "lnp", [DEPTH, 128, 6 * KC])
    f1w13 = dram_in(nc, "f1w13", [2, 22, 128, KC, 256], BF16)
    f1w2 = dram_in(nc, "f1w2", [8, 4, 128, 11, 256], BF16)
    winf = dram_in(nc, "winf", [14, 128, KC, 256], BF16)
    winv = dram_in(nc, "winv", [2, 128, KC, 512], BF16)
    cs128 = dram_in(nc, "cs128", [128, 256])
    x1T = dram_out(nc, "x1T", [128, KC, NT])
    outs = decl_proj_outs(nc)
    with ExitStack() as es:
        kb = KB(nc, es)
        c = setup_common(kb)
        alloc_p67(kb, c)
        x = kb.sb("x", [128, KC, NT], F32, nres=KC)
        kb.dma("sp", x.t[:], xT, wr=x.rs)
        load_mod(kb, c, mod, lnp, 0)
        with ExitStack() as es2:
            kb.es = es2
            bufs = make_ffn_bufs(kb)
            emit_ffn(kb, c, x, 0, f1w13, f1w2, FFN_PASSES, bufs)
            kb.barrier()
            kb.es = es
        kb.dma("sp", x1T, x.t[:], rd=x.rs)
        emit_proj(kb, c, x, winf, winv, cs128, outs)
        kb.finish()
    return nc


def tile_w13(w1, w3):
    def t(w):
        return w.reshape(KC, 128, 22, 256).transpose(2, 1, 0, 3)
    return np.ascontiguousarray(np.stack([t(w1), t(w3)], axis=0))


def tile_w2(w2):
    return np.ascontiguousarray(w2.reshape(4, 11, 128, 8, 256).transpose(3, 0, 2, 1, 4))


def tile_win(w_in):
    q = w_in[:, 0:1024]
    k = w_in[:, 1024:2048]
    v = w_in[:, 2048:3072]
    f = w_in[:, 3072:3584]
    a = w_in[:, 3584:4096]
    g = w_in[:, 4096:4608]
    ag = np.concatenate([np.concatenate([a[:, 128 * i:128 * i + 128], g[:, 128 * i:128 * i + 128]], axis=1) for i in range(4)], axis=1)
    fm = np.concatenate([q, k, f, ag], axis=1)
    winf = np.ascontiguousarray(fm.reshape(KC, 128, 14, 256).transpose(2, 1, 0, 3))
    winv = np.ascontiguousarray(v.reshape(KC, 128, 2, 512).transpose(2, 1, 0, 3))
    return winf, winv


def dft_cs128():
    k = np.arange(128)
    ang = 2.0 * np.pi * np.outer(k, k) / 128.0
    return np.ascontiguousarray(np.concatenate([np.cos(ang), np.sin(ang)], axis=1) / np.sqrt(128.0), dtype=np.float32)


def to_fm(a):
    return a.T.reshape(KC, 128, a.shape[0]).transpose(1, 0, 2)


def make_lnp(ln_g, ln_b):
    L = ln_g.shape[0]
    st = np.stack([ln_g[:, 0], ln_b[:, 0], ln_g[:, 1], ln_b[:, 1], ln_g[:, 2], ln_b[:, 2]], axis=1)
    return np.ascontiguousarray(st.reshape(L, 6, KC, 128).transpose(0, 3, 1, 2).reshape(L, 128, 6 * KC))


ATT_WIN = [(0, 12, 0), (2, 10, 1024), (4, 9, 1920), (6, 9, 1920), (8, 9, 1920), (10, 9, 1920), (12, 9, 2752), (12, 11, 3584)]
NBIAS = 4544
NKH = 1536 + CTX
SCALE = 128 ** -0.5


def emit_attn(kb, c, aps, ynaT, with_ctx_q):
    P = c.P
    PT = c.PT
    q_t = [kb.sb("at_q%d" % i, [128, NT], BF16) for i in range(2)]
    k_t = [kb.sb("at_k%d" % i, [128, NKH], BF16) for i in range(2)]
    v_t = [kb.sb("at_v%d" % i, [128, 14, 128], BF16) for i in range(2)]
    b_t = [kb.sb("at_b%d" % i, [128, NBIAS], F32) for i in range(2)]
    s_t = [kb.sb("at_s%d" % i, [128, 1024], F32) for i in range(2)]
    p_t = [kb.sb("at_p%d" % i, [128, 1024], BF16) for i in range(2)]
    pT_t = [kb.sb("at_pT%d" % i, [128, 8, 128], BF16) for i in range(2)]
    on_t = [kb.sb("at_on%d" % i, [128, 128], BF16) for i in range(2)]
    st_t = [kb.sb("at_st%d" % i, [128, 4], F32) for i in range(2)]
    ident = kb.sb("at_id", [128, 128], BF16)
    idf = kb.sb("at_idf", [128, 128], F32)
    kb.dma("sp", idf.t[:], aps["ident"], wr=[idf.r])
    kb.op("dve", lambda e: e.tensor_copy(out=ident.t[:], in_=idf.t[:]), rd=[idf.r], wr=[ident.r])

    def load(h):
        b = h % 2
        kb.dma("sp", q_t[b].t[:], aps["qT"][:, h, :], wr=[q_t[b].r])
        kb.dma("sp", k_t[b].t[:], aps["kTh"][:, h, :], wr=[k_t[b].r])
        kb.dma("sp", v_t[b].t[:], aps["Vh"][h], wr=[v_t[b].r])
        kb.dma("sp", b_t[b].t[:], aps["bias"][h], wr=[b_t[b].r])
    load(0)

    def geom(lt):
        if lt < 8:
            blo, nrow, boff = ATT_WIN[lt]
            return blo, 64 * nrow, boff, 128, 128 * lt
        return 0, 0, 0, TCX, TL

    def stage_a(u, h, lt):
        b0 = 3 * (u % 2)
        qh, kh, bh = q_t[h % 2], k_t[h % 2], b_t[h % 2]
        s, p, stt = s_t[u % 2], p_t[u % 2], st_t[u % 2]
        blo, nk, boff, nq, q0 = geom(lt)
        ntot = nk + CTX
        n1 = nk - 512

        def a1():
            if lt < 8:
                kb.op("pe", lambda e: e.matmul(P[b0].t[:, 0:512], qh.t[:, q0:q0 + 128], kh.t[:, 64 * blo:64 * blo + 512], start=True, stop=True),
                      rd=[qh.r, kh.r], wr=[P[b0].r])
                kb.op("pe", lambda e: e.matmul(P[b0 + 1].t[:, 0:n1], qh.t[:, q0:q0 + 128],
                                               kh.t[:, 64 * blo + 512:64 * blo + nk], start=True, stop=True),
                      rd=[qh.r, kh.r], wr=[P[b0 + 1].r])
                kb.op("pe", lambda e: e.matmul(P[b0 + 1].t[:, n1:n1 + CTX], qh.t[:, q0:q0 + 128], kh.t[:, 1536:NKH], start=True, stop=True),
                      rd=[qh.r, kh.r], wr=[P[b0 + 1].r])
            else:
                kb.op("pe", lambda e: e.matmul(P[b0 + 1].t[0:nq, 0:CTX], qh.t[:, q0:q0 + nq], kh.t[:, 1536:NKH], start=True, stop=True),
                      rd=[qh.r, kh.r], wr=[P[b0 + 1].r])

        def a2():
            if lt < 8:
                kb.op("dve", lambda e: e.scalar_tensor_tensor(out=s.t[:, 0:512], in0=P[b0].t[:, 0:512], scalar=SCALE,
                                                              in1=bh.t[:, boff:boff + 512], op0=ALU.mult, op1=ALU.add),
                      rd=[P[b0].r, bh.r], wr=[s.r])
                kb.op("dve", lambda e: e.scalar_tensor_tensor(out=s.t[:, 512:ntot], in0=P[b0 + 1].t[:, 0:ntot - 512], scalar=SCALE,
                                                              in1=bh.t[:, boff + 512:boff + ntot], op0=ALU.mult, op1=ALU.add),
                      rd=[P[b0 + 1].r, bh.r], wr=[s.r])
            else:
                kb.op("act", lambda e: e.activation(out=s.t[0:nq, 0:CTX], in_=P[b0 + 1].t[0:nq, 0:CTX], func=AF.Copy, scale=SCALE),
                      rd=[P[b0 + 1].r], wr=[s.r])

        def a3():
            kb.op("dve", lambda e: e.reduce_max(out=stt.t[0:nq, 1:2], in_=s.t[0:nq, 0:ntot], axis=mybir.AxisListType.X, negate=True),
                  rd=[s.r], wr=[stt.r])

        def a4():
            kb.op("act", lambda e: e.activation(out=p.t[0:nq, 0:ntot], in_=s.t[0:nq, 0:ntot], func=AF.Exp, bias=stt.t[0:nq, 1:2], scale=1.0,
                                                accum_out=stt.t[0:nq, 2:3]),
                  rd=[s.r, stt.r], wr=[p.r, stt.r])

        def a5():
            kb.op("dve", lambda e: e.reciprocal(out=stt.t[0:nq, 3:4], in_=stt.t[0:nq, 2:3]), rd=[stt.r], wr=[stt.r])
        return [a1, a2, a3, a4, a5]

    def stage_b(u, h, lt):
        b0 = 3 * (u % 2)
        vh = v_t[h % 2]
        p, pT, on, stt = p_t[u % 2], pT_t[u % 2], on_t[u % 2], st_t[u % 2]
        ptb = PT[u % 2]
        blo, nk, boff, nq, q0 = geom(lt)
        chunks = []
        for ci in range((nk + 127) // 128):
            chunks.append((128 * ci, min(128, nk - 128 * ci), blo // 2 + ci))
        chunks += [(nk, 128, 12), (nk + 128, 128, 13)]
        nch = len(chunks)
        o_ps = P[b0 + 2]

        def b1():
            for ci, (c0, w, vc) in enumerate(chunks):
                kb.op("pe", lambda e: e.transpose(ptb.t[0:w, ci, 0:nq], p.t[0:nq, c0:c0 + w], ident.t[0:nq, 0:nq]),
                      rd=[p.r, ident.r], wr=[ptb.r])

        def b2():
            kb.op("dve", lambda e: e.tensor_copy(out=pT.t[:, 0:nch, 0:nq], in_=ptb.t[:, 0:nch, 0:nq]), rd=[ptb.r], wr=[pT.r])

        def b3():
            for ci, (c0, w, vc) in enumerate(chunks):
                kb.op("pe", lambda e: e.matmul(o_ps.t[0:nq, 0:128], pT.t[0:w, ci, 0:nq], vh.t[0:w, vc, :],
                                               start=(ci == 0), stop=(ci == nch - 1)), rd=[pT.r, vh.r], wr=[o_ps.r])

        def b4():
            kb.op("dve", lambda e: e.tensor_scalar(out=on.t[0:nq, :], in0=o_ps.t[0:nq, 0:128], scalar1=stt.t[0:nq, 3:4], scalar2=None,
                                                   op0=ALU.mult), rd=[o_ps.r, stt.r], wr=[on.r])

        def b5():
            kb.op("pe", lambda e: e.transpose(ptb.t[:, 0, 0:nq], on.t[0:nq, :], ident.t[0:nq, 0:nq]), rd=[on.r, ident.r], wr=[ptb.r])

        def b6():
            kb.op("act", lambda e: e.copy(out=ynaT.t[:, h, q0:q0 + nq], in_=ptb.t[:, 0, 0:nq]), rd=[ptb.r], wr=[ynaT.rs[h]])
        return [b1, b2, b3, b4, b5, b6]

    tiles = list(range(8)) + ([8] if with_ctx_q else [])
    units = [(h, lt) for h in range(8) for lt in tiles]
    prev = None
    for u, (h, lt) in enumerate(units):
        a = stage_a(u, h, lt)
        b = stage_b(*prev) if prev is not None else [lambda: None] * 6
        a[0](); b[0](); a[1](); b[1](); a[2](); b[2](); a[3](); b[3](); b[4](); a[4](); b[5]()
        if lt == 0 and h + 1 < 8:
            load(h + 1)
        prev = (u, h, lt)
    for f in stage_b(*prev):
        f()


def emit_fft(kb, c, aps, yfT, with_ctx):
    P = c.P
    NB = 3
    xt = [kb.sb("ff_x%d" % i, [128, 1024], BF16) for i in range(NB)]
    xr = [kb.sb("ff_r%d" % i, [128, 1024], BF16) for i in range(NB)]
    eo = [kb.sb("ff_e%d" % i, [128, 1024], BF16) for i in range(NB)]
    cn = [kb.sb("ff_c%d" % i, [128, 2, 1024], BF16) for i in range(NB)]
    cnc = kb.sb("ff_cc", [128, 2, TCX], BF16)
    xn = kb.sb("ff_xn", [1, 1024], BF16)
    alt = kb.sb("ff_alt", [1, 1024], BF16)
    kb.dma("sp", xn.t[:], aps["xnyq"], wr=[xn.r])
    kb.dma("sp", alt.t[:], aps["alt"], wr=[alt.r])
    NTC = SEQ // 2 // 128

    def load(tc):
        b = tc % NB
        kb.dma("sp", xt[b].t[:], aps["Xall"][128 * tc:128 * tc + 128, :], wr=[xt[b].r])
        kb.dma("sp", xr[b].t[:], aps["Xrev"][128 * tc:128 * tc + 128, :], wr=[xr[b].r])
        kb.dma("sp", cn[b].t[:], aps["cn"][tc], wr=[cn[b].r])
    load(0)
    load(1)
    for tc in range(NTC):
        if tc + 2 < NTC:
            load(tc + 2)
        b = tc % NB
        x_, r_, e_, c_ = xt[b], xr[b], eo[b], cn[b]
        kb.op("pool", lambda e: e.tensor_tensor(out=e_.t[:, 0:512], in0=x_.t[:, 0:512], in1=r_.t[:, 0:512], op=ALU.add),
              rd=[x_.r, r_.r], wr=[e_.r])
        kb.op("pool", lambda e: e.tensor_tensor(out=e_.t[:, 512:1024], in0=x_.t[:, 512:1024], in1=r_.t[:, 512:1024], op=ALU.subtract),
              rd=[x_.r, r_.r], wr=[e_.r])
        for m in range(4):
            for kh in range(2):
                acc = P[2 * m + kh]
                for w in range(2):
                    kb.op("pe", lambda e: e.matmul(acc.t[:, :], e_.t[:, 512 * w + 128 * m:512 * w + 128 * m + 128],
                                                   c_.t[:, w, 512 * kh:512 * kh + 512],
                                                   start=(tc == 0 and w == 0), stop=False),
                          rd=[e_.r, c_.r], wr=[acc.r])
    for m in range(4):
        for kh in range(2):
            acc = P[2 * m + kh]
            kb.op("pe", lambda e: e.matmul(acc.t[:, :], xn.t[0:1, 128 * m:128 * m + 128], alt.t[0:1, 512 * kh:512 * kh + 512],
                                           start=False, stop=True), rd=[xn.r, alt.r], wr=[acc.r])
    for m in range(4):
        for kh in range(2):
            acc = P[2 * m + kh]
            if kh == 0:
                kb.op("act", lambda e: e.copy(out=yfT.t[:, m, 512 * kh:512 * kh + 512], in_=acc.t[:, :]), rd=[acc.r], wr=[yfT.rs[m]])
            else:
                kb.op("dve", lambda e: e.tensor_copy(out=yfT.t[:, m, 512 * kh:512 * kh + 512], in_=acc.t[:, :]), rd=[acc.r], wr=[yfT.rs[m]])
    if with_ctx:
        for tc in range(2):
            x_ = xt[tc]
            kb.dma("sp", x_.t[:], aps["Xctx"][128 * tc:128 * tc + 128, :], wr=[x_.r])
            kb.dma("sp", cnc.t[:], aps["cnc"][tc], wr=[cnc.r])
            for m in range(4):
                for w in range(2):
                    kb.op("pe", lambda e: e.matmul(P[m].t[:, 0:TCX], x_.t[:, 512 * w + 128 * m:512 * w + 128 * m + 128], cnc.t[:, w, :],
                                                   start=(tc == 0 and w == 0), stop=(tc == 1 and w == 1)),
                          rd=[x_.r, cnc.r], wr=[P[m].r])
        for m in range(4):
            kb.op("act", lambda e: e.copy(out=yfT.t[:, m, TL:NT], in_=P[m].t[:, 0:TCX]), rd=[P[m].r], wr=[yfT.rs[m]])


def emit_conv(kb, c, aps, ycT, with_ctx, lntmp):
    ut = kb.sb("cv_u", [128, 4, TL + 30], F32)
    uc = kb.sb("cv_uc", [128, 4, TCX + 30], F32)
    acc = kb.sb("cv_acc", [128, 4, NT], F32, nres=4)
    wd = kb.sb("cv_w", [128, 4, 31], F32)
    cp = kb.sb("cv_p", [128, 4, 3], F32)
    ones4 = kb.sb("cv_ones", [128, 128], F32)
    kb.op("dve", lambda e: e.memset(ones4.t[:], 1.0 / 512.0), wr=[ones4.r])
    kb.dma("sp", ut.t[:], aps["uTh"], wr=[ut.r])
    kb.dma("sp", uc.t[:], aps["uTc"], wr=[uc.r])
    kb.dma("sp", wd.t[:], aps["wdw"], wr=[wd.r])
    kb.dma("sp", cp.t[:], aps["cvp"], wr=[cp.r])
    srcs = [(ut, 0, TL)] + ([(uc, TL, TCX)] if with_ctx else [])
    for ci in range(4):
        eng = "dve"
        for (src, c0, n) in srcs:
            kb.op(eng, lambda e: e.tensor_scalar(out=acc.t[:, ci, c0:c0 + n], in0=src.t[:, ci, 0:n], scalar1=wd.t[:, ci, 0:1],
                                                 scalar2=cp.t[:, ci, 0:1], op0=ALU.mult, op1=ALU.add),
                  rd=[src.r, wd.r, cp.r], wr=[acc.rs[ci]])
            for j in range(1, 31):
                kb.op(eng, lambda e: e.scalar_tensor_tensor(out=acc.t[:, ci, c0:c0 + n], in0=src.t[:, ci, j:j + n], scalar=wd.t[:, ci, j:j + 1],
                                                            in1=acc.t[:, ci, c0:c0 + n], op0=ALU.mult, op1=ALU.add),
                      rd=[src.r, wd.r, acc.rs[ci]], wr=[acc.rs[ci]])
    def finish_ln():
        blks = ALL_BLKS if with_ctx else ALL_BLKS[:2]
        emit_ln_g(kb, c, acc, 4, blks, lntmp, ones4, LN_EPS, lambda kc: cp.t[:, kc, 1:2], lambda kc: cp.t[:, kc, 2:3], [cp.r],
                  out=ycT, func=AF.Silu)
    return finish_ln


def emit_merge(kb, c, x, ynaT, yfT, ycT, wgb_ap, wo_ap, bg_ap, with_ctx, lntmp):
    P = c.P
    h2 = kb.sb("mg_h2", [128, KC, 512], BF16, nres=KC)
    mT = kb.sb("mg_m", [128, KC, 512], BF16, nres=KC)
    wgb = [kb.sb("mg_w%d" % i, [128, 64, 128], BF16) for i in range(2)]
    wo = [kb.sb("mg_wo%d" % i, [128, KC, 128], BF16) for i in range(2)]
    gt = [kb.sb("mg_g%d" % i, [128, 512], F32) for i in range(3)]
    mt = [kb.sb("mg_t%d" % i, [128, 512], F32) for i in range(3)]
    bg = kb.sb("mg_bg", [128, 3, KC], F32)
    kb.dma("sp", bg.t[:], bg_ap, wr=[bg.r])
    blks = ALL_BLKS if with_ctx else ALL_BLKS[:2]
    ys = [(ynaT, 8, 48), (yfT, 4, 56), (ycT, 4, 60)]
    for (c0, n) in blks:
        w = blk_w(c0)
        for kc in range(KC):
            if kc % 2:
                kb.op("dve", lambda e: e.tensor_scalar(out=h2.t[:, kc, 0:n], in0=x.t[:, kc, c0:c0 + n], scalar1=c.sc1.t[:, 1, kc, w:w + 1],
                                                       scalar2=c.mod.t[:, 3, kc, w:w + 1], op0=ALU.mult, op1=ALU.add),
                      rd=[x.rs[kc], c.sc1.r, c.mod.r], wr=[h2.rs[kc]])
            else:
                kb.op("act", lambda e: e.activation(out=h2.t[:, kc, 0:n], in_=x.t[:, kc, c0:c0 + n], func=AF.Identity,
                                                    scale=c.sc1.t[:, 1, kc, w:w + 1], bias=c.mod.t[:, 3, kc, w:w + 1]),
                      rd=[x.rs[kc], c.sc1.r, c.mod.r], wr=[h2.rs[kc]])
        kb.dma("pool", wgb[0].t[:], wgb_ap[0], wr=[wgb[0].r])
        for d in range(16):
            if d + 1 < 16:
                kb.dma("pool", wgb[(d + 1) % 2].t[:], wgb_ap[d + 1], wr=[wgb[(d + 1) % 2].r])
            wt = wgb[d % 2]
            for br in range(3):
                for kc in range(KC):
                    kb.op("pe", lambda e: e.matmul(P[br].t[:, 0:n], wt.t[:, br * KC + kc, :], h2.t[:, kc, 0:n],
                                                   start=(kc == 0), stop=(kc == KC - 1)), rd=[wt.r, h2.rs[kc]], wr=[P[br].r])
            for br, (yt, nk, off) in enumerate(ys):
                for k in range(nk):
                    kb.op("pe", lambda e: e.matmul(P[3 + br].t[:, 0:n], wt.t[:, off + k, :], yt.t[:, k, c0:c0 + n],
                                                   start=(k == 0), stop=(k == nk - 1)), rd=[wt.r, yt.rs[k]], wr=[P[3 + br].r])
            for br in range(3):
                kb.op("act", lambda e: e.activation(out=gt[br].t[:, 0:n], in_=P[br].t[:, 0:n], func=AF.Sigmoid,
                                                    bias=bg.t[:, br, d:d + 1], scale=1.0), rd=[P[br].r, bg.r], wr=[gt[br].r])
                kb.op("dve", lambda e: e.tensor_tensor(out=mt[br].t[:, 0:n], in0=gt[br].t[:, 0:n], in1=P[3 + br].t[:, 0:n], op=ALU.mult),
                      rd=[gt[br].r, P[3 + br].r], wr=[mt[br].r])
            kb.op("dve", lambda e: e.tensor_tensor(out=mt[0].t[:, 0:n], in0=mt[0].t[:, 0:n], in1=mt[1].t[:, 0:n], op=ALU.add),
                  rd=[mt[0].r, mt[1].r], wr=[mt[0].r])
            kb.op("dve", lambda e: e.tensor_tensor(out=mT.t[:, d, 0:n], in0=mt[0].t[:, 0:n], in1=mt[2].t[:, 0:n], op=ALU.add),
                  rd=[mt[0].r, mt[2].r], wr=[mT.rs[d]])
        kb.dma("pool", wo[0].t[:], wo_ap[0], wr=[wo[0].r])
        for d in range(16):
            if d + 1 < 16:
                kb.dma("pool", wo[(d + 1) % 2].t[:], wo_ap[d + 1], wr=[wo[(d + 1) % 2].r])
            wt = wo[d % 2]
            pt = P[d % 2]
            for kc in range(KC):
                kb.op("pe", lambda e: e.matmul(pt.t[:, 0:n], wt.t[:, kc, :], mT.t[:, kc, 0:n], start=(kc == 0), stop=(kc == KC - 1)),
                      rd=[wt.r, mT.rs[kc]], wr=[pt.r])
            kb.op("dve", lambda e: e.scalar_tensor_tensor(out=x.t[:, d, c0:c0 + n], in0=pt.t[:, 0:n], scalar=c.gsc.t[:, 1, d, w:w + 1],
                                                          in1=x.t[:, d, c0:c0 + n], op0=ALU.mult, op1=ALU.add),
                  rd=[pt.r, c.gsc.r, x.rs[d]], wr=[x.rs[d]])
    emit_ln(kb, c, x, 1, blks, lntmp)


def build_stage23(last, debug=False):
    nc = bass.Bass("TRN2", target_bir_lowering=False)
    if debug:
        dbg = {"yna": dram_out(nc, "d_yna", [128, 8, NT], BF16), "yf": dram_out(nc, "d_yf", [128, 4, NT], BF16),
               "yc": dram_out(nc, "d_yc", [128, 4, NT], BF16), "x2": dram_out(nc, "d_x2", [128, KC, NT])}
    l = 1 if last else 0
    xin = dram_in(nc, "x1T", [128, KC, NT])
    mod = dram_in(nc, "mod", [DEPTH, 128, 9 * KC * 2])
    lnp = dram_in(nc, "lnp", [DEPTH, 128, 6 * KC])
    at = {"qT": dram_in(nc, "qT", [128, 8, NT], BF16), "kTh": dram_in(nc, "kTh", [128, 8, NKH], BF16),
          "Vh": dram_in(nc, "Vh", [8, 128, 14, 128], BF16), "bias": dram_in(nc, "bias", [8, 128, NBIAS]),
          "ident": dram_in(nc, "ident", [128, 128])}
    ff = {"Xall": dram_in(nc, "Xall", [SEQ, 1024], BF16), "cn": dram_in(nc, "cn", [32, 128, 2, 1024], BF16),
          "Xrev": dram_in(nc, "Xrev", [SEQ // 2, 1024], BF16), "xnyq": dram_in(nc, "xnyq", [1, 1024], BF16),
          "alt": dram_in(nc, "alt", [1, 1024], BF16),
          "Xctx": dram_in(nc, "Xctx", [CTX, 1024], BF16), "cnc": dram_in(nc, "cnc", [2, 128, 2, TCX], BF16)}
    cv = {"uTh": dram_in(nc, "uTh", [128, 4, TL + 30]), "uTc": dram_in(nc, "uTc", [128, 4, TCX + 30]),
          "wdw": dram_in(nc, "wdw", [128, 4, 31]), "cvp": dram_in(nc, "cvp", [128, 4, 3])}
    wgb = dram_in(nc, "wgb", [16, 128, 64, 128], BF16)
    wo = dram_in(nc, "wo", [16, 128, KC, 128], BF16)
    bg = dram_in(nc, "bg", [128, 3, KC])
    f2w13 = dram_in(nc, "f2w13", [2, 22, 128, KC, 256], BF16)
    f2w2 = dram_in(nc, "f2w2", [8, 4, 128, 11, 256], BF16)
    if not last:
        f1w13 = dram_in(nc, "f1w13", [2, 22, 128, KC, 256], BF16)
        f1w2 = dram_in(nc, "f1w2", [8, 4, 128, 11, 256], BF16)
        winf = dram_in(nc, "winf", [14, 128, KC, 256], BF16)
        winv = dram_in(nc, "winv", [2, 128, KC, 512], BF16)
        cs128 = dram_in(nc, "cs128", [128, 256])
        outs = decl_proj_outs(nc)
    xout = dram_out(nc, "xoT", [128, KC, NT])
    wc = not last
    with ExitStack() as es:
        kb = KB(nc, es)
        c = setup_common(kb)
        x = kb.sb("x", [128, KC, NT], F32, nres=KC)
        kb.dma("sp", x.t[:], xin, wr=x.rs)
        load_mod(kb, c, mod, lnp, l)
        with ExitStack() as es2:
            kb.es = es2
            ynaT = kb.sb("ynaT", [128, 8, NT], BF16, nres=8)
            yfT = kb.sb("yfT", [128, 4, NT], BF16, nres=4)
            ycT = kb.sb("ycT", [128, 4, NT], BF16, nres=4)
            with ExitStack() as es3:
                kb.es = es3
                alloc_pt(kb, c)
                if "attn" not in SKIP:
                    emit_attn(kb, c, at, ynaT, wc)
                kb.barrier()
                kb.es = es2
            alloc_p67(kb, c)
            lntmp = make_ln_tmp(kb)
            with ExitStack() as es3:
                kb.es = es3
                fin = emit_conv(kb, c, cv, ycT, wc, lntmp) if "conv" not in SKIP else None
                if "fft" not in SKIP:
                    emit_fft(kb, c, ff, yfT, wc)
                if fin is not None:
                    fin()
                kb.barrier()
                kb.es = es2
            with ExitStack() as es3:
                kb.es = es3
                if debug:
                    kb.dma("sp", dbg["yna"], ynaT.t[:], rd=ynaT.rs)
                    kb.dma("sp", dbg["yf"], yfT.t[:], rd=yfT.rs)
                    kb.dma("sp", dbg["yc"], ycT.t[:], rd=ycT.rs)
                if "merge" not in SKIP:
                    emit_merge(kb, c, x, ynaT, yfT, ycT, wgb, wo, bg, wc, lntmp)
                if debug:
                    kb.dma("sp", dbg["x2"], x.t[:], rd=x.rs)
                kb.barrier()
                kb.es = es2
            kb.barrier()
            kb.es = es
        with ExitStack() as es2:
            kb.es = es2
            alloc_p67(kb, c)
            with ExitStack() as es3:
                kb.es = es3
                bufs = make_ffn_bufs(kb)
                passes = FFN_PASSES if wc else [[(0, 512)], [(512, 512)]]
                if "ffn2" not in SKIP:
                    emit_ffn(kb, c, x, 2, f2w13, f2w2, passes, bufs)
                kb.barrier()
                kb.es = es2
            if not last:
                load_mod(kb, c, mod, lnp, 1)
                with ExitStack() as es3:
                    kb.es = es3
                    bufs = make_ffn_bufs_named(kb, "b")
                    emit_ffn(kb, c, x, 0, f1w13, f1w2, FFN_PASSES, bufs)
                    kb.barrier()
                    kb.es = es2
                kb.dma("sp", xout, x.t[:], rd=x.rs)
                emit_proj(kb, c, x, winf, winv, cs128, outs)
            else:
                kb.dma("sp", xout, x.t[:], rd=x.rs)
            kb.barrier()
            kb.es = es
        kb.finish()
    return nc


def make_ffn_bufs_named(kb, sfx):
    b = {}
    b["h"] = kb.sb("ffn_h" + sfx, [128, KC, 544], BF16, nres=KC)
    b["g"] = kb.sb("ffn_g" + sfx, [128, NJ, 544], BF16, nres=NJ)
    b["w13"] = [[kb.sb("w13_%d_%d%s" % (s, i, sfx), [128, KC, 256], BF16) for i in range(2)] for s in range(2)]
    b["w2"] = [kb.sb("w2_%d%s" % (i, sfx), [128, 11, 256], BF16) for i in range(2)]
    b["sa"] = [kb.sb("sa%d%s" % (i, sfx), [128, 544], F32) for i in range(2)]
    b["ln"] = {"sq": [kb.sb("ln_sq%d%s" % (i, sfx), [128, 512], F32) for i in range(2)],
               "t1": [kb.sb("ln_t1%d%s" % (i, sfx), [128, 512], F32) for i in range(2)]}
    for nm in ("mean", "var", "rstd", "nmr"):
        b["ln"][nm] = kb.sb("ln_" + nm + sfx, [128, 512], F32)
    return b


def build_bias(rpb_l, i):
    out = np.full((8, 128, NBIAS), NEG, np.float32)
    cq = np.arange(64)
    c0 = np.clip(cq - 8, 0, 48)
    col_ok = (cq[None, :] >= c0[:, None]) & (cq[None, :] < c0[:, None] + 16)
    dc_i = np.clip(cq[None, :] - cq[:, None], -15, 15) + 15
    for lt in (0, 1, 2, 6, 7):
        blo, nrow, boff = ATT_WIN[lt]
        for qr in range(2):
            r = 16 * i + 2 * lt + qr
            r0 = min(max(r - 4, 0), 120)
            for bb in range(nrow):
                kr = 16 * i - 4 + blo + bb
                if 0 <= kr < 128 and r0 <= kr < r0 + 8:
                    vals = rpb_l[:, kr - r + 7, :][:, dc_i]
                    out[:, 64 * qr:64 * qr + 64, boff + 64 * bb:boff + 64 * bb + 64] = np.where(col_ok[None], vals, np.float32(NEG))
        out[:, :, boff + 64 * nrow:boff + 64 * nrow + CTX] = 0.0
    return out


def dft_tables(i):
    t = np.arange(SEQ // 2, dtype=np.int64)[:, None]
    k = (1024 * i + np.arange(1024, dtype=np.int64))[None, :]
    ang = 2.0 * np.pi * ((t * k) % SEQ).astype(np.float64) / SEQ
    cn = np.stack([np.cos(ang), -np.sin(ang)], axis=1) / np.sqrt(float(SEQ))
    cn = np.ascontiguousarray(cn.reshape(32, 128, 2, 1024).astype(np.float32).astype(NPBF))
    alt = np.where((1024 * i + np.arange(1024)) % 2 == 0, 1.0, -1.0) / np.sqrt(float(SEQ))
    alt = np.ascontiguousarray(alt.reshape(1, 1024).astype(np.float32).astype(NPBF))
    t = np.arange(CTX, dtype=np.int64)[:, None]
    k = (TCX * i + np.arange(TCX, dtype=np.int64))[None, :]
    ang = 2.0 * np.pi * ((t * k) % CTX).astype(np.float64) / CTX
    cc = np.stack([np.cos(ang), -np.sin(ang)], axis=1) / np.sqrt(float(CTX))
    cc = np.ascontiguousarray(cc.reshape(2, 128, 2, TCX).astype(np.float32).astype(NPBF))
    return cn, cc, alt


def exchange(res, l, inp, consts):
    R = [{k: np.asarray(v) for k, v in r.items()} for r in res]
    kT = np.concatenate([r["o_kT"][:, :, :TL] for r in R], axis=2)
    kTc = np.concatenate([r["o_kT"][:, :, TL:] for r in R], axis=2)
    kpad = np.concatenate([np.zeros((128, 8, 256), kT.dtype), kT, np.zeros((128, 8, 256), kT.dtype)], axis=2)
    V = np.concatenate([r["o_V"][:TL] for r in R], axis=0)
    Vc = np.concatenate([r["o_V"][TL:] for r in R], axis=0)
    Vpad = np.concatenate([np.zeros((256, 1024), V.dtype), V, np.zeros((256, 1024), V.dtype)], axis=0)
    Xall = np.ascontiguousarray(np.concatenate([r["o_XcXs"][:TL] for r in R], axis=0))
    Xctx = np.ascontiguousarray(np.concatenate([r["o_XcXs"][TL:] for r in R], axis=0))
    Xrev = np.ascontiguousarray(np.concatenate([np.zeros((1, 1024), Xall.dtype), Xall[:SEQ // 2:-1]], axis=0))
    assert Xrev.shape[0] == SEQ // 2
    xnyq = np.ascontiguousarray(Xall[SEQ // 2:SEQ // 2 + 1])
    uT = np.concatenate([r["o_uT"][:, :, :TL] for r in R], axis=2)
    uTc = np.concatenate([r["o_uT"][:, :, TL:] for r in R], axis=2)
    upad = np.concatenate([np.zeros((128, 4, 15), uT.dtype), uT, np.zeros((128, 4, 15), uT.dtype)], axis=2)
    ucpad = np.concatenate([np.zeros((128, 4, 15), uT.dtype), uTc, np.zeros((128, 4, 15), uT.dtype)], axis=2)
    wdw = np.ascontiguousarray(inp["w_dw"][l].T.reshape(4, 128, 31).transpose(1, 0, 2))
    cvp = np.ascontiguousarray(np.stack([inp["b_dw"][l], inp["conv_ln_g"][l], inp["conv_ln_b"][l]], axis=-1).reshape(4, 128, 3).transpose(1, 0, 2))
    wb = consts["wb"][l]
    bgt = np.ascontiguousarray(inp["b_gate"][l].reshape(3, KC, 128).transpose(2, 0, 1))
    maps = []
    for i in range(NCORE):
        cn, cc, alt = consts["dft"][i]
        Vh = np.concatenate([Vpad[1024 * i:1024 * i + 1536], Vc], axis=0).reshape(14, 128, 8, 128).transpose(2, 1, 0, 3)
        m = {"x1T": R[i]["x1T"] if "x1T" in R[i] else R[i]["xoT"], "mod": consts["mod"], "lnp": consts["lnp"],
             "qT": R[i]["o_qT"], "kTh": np.ascontiguousarray(np.concatenate([kpad[:, :, 1024 * i:1024 * i + 1536], kTc], axis=2)),
             "Vh": np.ascontiguousarray(Vh), "bias": build_bias(inp["rpb"][l], i), "ident": np.eye(128, dtype=np.float32),
             "Xall": Xall, "cn": cn, "Xctx": Xctx, "cnc": cc, "Xrev": Xrev, "xnyq": xnyq, "alt": alt,
             "uTh": np.ascontiguousarray(upad[:, :, 1024 * i:1024 * i + TL + 30]),
             "uTc": np.ascontiguousarray(ucpad[:, :, TCX * i:TCX * i + TCX + 30]),
             "wdw": wdw, "cvp": cvp, "wgb": wb["wgb"], "wo": wb["wo"], "bg": bgt, "f2w13": wb["f2w13"], "f2w2": wb["f2w2"]}
        maps.append(m)
    return maps


def kernel(**inp):
    inp = {k: np.asarray(v) for k, v in inp.items()}
    cores = list(range(NCORE))
    modT, wb = run_mod(inp["c"][0], inp["c_ctx"], inp["w_ada"], inp["b_ada"], inp)
    consts = {"mod": modT, "wb": wb, "lnp": make_lnp(inp["ln_g"], inp["ln_b"]), "dft": [dft_tables(i) for i in range(NCORE)]}
    x = inp["x"][0]
    ctx = inp["ctx"][0]
    cs128 = dft_cs128()

    def nxt(l):
        return {"f1w13": wb[l]["f1w13"], "f1w2": wb[l]["f1w2"], "winf": wb[l]["winf"], "winv": wb[l]["winv"], "cs128": cs128}
    n0 = nxt(0)
    maps = []
    for i in cores:
        toks = np.concatenate([x[TL * i:TL * (i + 1)], ctx[TCX * i:TCX * (i + 1)]], axis=0)
        m = {"xT": np.ascontiguousarray(to_fm(toks)), "mod": consts["mod"], "lnp": consts["lnp"]}
        m.update(n0)
        maps.append(m)
    res = run_bass_kernel_spmd(build_stage1(), maps, core_ids=cores).results
    del n0, maps
    maps = exchange(res, 0, inp, consts)
    n1 = nxt(1)
    for m in maps:
        m.update(n1)
    res = run_bass_kernel_spmd(build_stage23(False), maps, core_ids=cores).results
    del n1, maps
    maps = exchange(res, 1, inp, consts)
    res = run_bass_kernel_spmd(build_stage23(True), maps, core_ids=cores).results
    out = np.zeros((1, SEQ, D), np.float32)
    for i in cores:
        xo = np.asarray(res[i]["xoT"])[:, :, :TL]
        out[0, TL * i:TL * (i + 1)] = xo.transpose(2, 1, 0).reshape(TL, D)
    return out
```

```python
from contextlib import ExitStack
import numpy as np
import ml_dtypes
import concourse.bass as bass
import concourse.mybir as mybir
from concourse.bass_utils import run_bass_kernel_spmd

F32 = mybir.dt.float32
BF16 = mybir.dt.bfloat16
AF = mybir.ActivationFunctionType
ALU = mybir.AluOpType
NPBF = ml_dtypes.bfloat16

D = 2048
KC = 16
SEQ = 8192
DEPTH = 2
NCORE = 8
TL = 1024
TCX = 32
NT = TL + TCX
CTX = 256
DFF = 5632
NJ = 44
DNA = 1024
DIN = 4608
GRID_W = 64
ALPHA = (2 * DEPTH) ** 0.25
LN_EPS = 1e-5
EPS_S = LN_EPS / (ALPHA * ALPHA)
NEG = -1e30
WQ = "sp"
SKIP = set()


class R:
    __slots__ = ("w", "rd")

    def __init__(self):
        self.w = None
        self.rd = {}


class T:
    def __init__(self, t, nres=1):
        self.t = t
        self.rs = [R() for _ in range(nres)]

    @property
    def r(self):
        return self.rs[0]


class KB:
    NDMA = 20

    def __init__(self, nc, es):
        self.nc = nc
        self.es = es
        self.eng = {"pe": nc.tensor, "act": nc.scalar, "dve": nc.vector, "pool": nc.gpsimd, "sp": nc.sync}
        self.psem = {e: es.enter_context(nc.semaphore("p_" + e)) for e in ("pe", "act", "dve", "pool")}
        self.pcnt = {e: 0 for e in self.psem}
        self.seen = {e: {} for e in self.eng}
        self.dsem = [es.enter_context(nc.semaphore("d%d" % i)) for i in range(self.NDMA)]
        self.dcnt = [0] * self.NDMA
        self.rr = 0

    uid = 0

    def sb(self, name, shape, dt, nres=1):
        KB.uid += 1
        return T(self.es.enter_context(self.nc.sbuf_tensor("%s_%d" % (name, KB.uid), list(shape), dt)), nres)

    def ps(self, name, shape, dt=F32, nres=1):
        KB.uid += 1
        return T(self.es.enter_context(self.nc.psum_tensor("%s_%d" % (name, KB.uid), list(shape), dt)), nres)

    def _waits(self, e, rd, wr, extra=None):
        need = dict(extra or {})
        for r in rd:
            if r.w is not None and need.get(r.w[0], 0) < r.w[1]:
                need[r.w[0]] = r.w[1]
        for r in wr:
            if r.w is not None and need.get(r.w[0], 0) < r.w[1]:
                need[r.w[0]] = r.w[1]
            for s, v in r.rd.items():
                if need.get(s, 0) < v:
                    need[s] = v
        eng = self.eng[e]
        seen = self.seen[e]
        own = self.psem.get(e)
        for s, v in need.items():
            if e == "pe" and s is own:
                continue
            if seen.get(s, 0) < v:
                eng.wait_ge(s, v)
                seen[s] = v

    def _mark(self, tok, rd, wr):
        s, v = tok
        for r in rd:
            if r.rd.get(s, 0) < v:
                r.rd[s] = v
        for r in wr:
            r.w = tok
            r.rd = {}

    def op(self, e, fn, rd=(), wr=()):
        self._waits(e, rd, wr)
        ins = fn(self.eng[e])
        self.pcnt[e] += 1
        ins.then_inc(self.psem[e], 1)
        self._mark((self.psem[e], self.pcnt[e]), rd, wr)

    def dma(self, e, out, in_, rd=(), wr=()):
        k = self.rr
        self.rr = (self.rr + 1) % self.NDMA
        extra = {self.dsem[k]: self.dcnt[k]} if self.dcnt[k] else None
        self._waits(e, rd, wr, extra)
        ins = self.eng[e].dma_start(out=out, in_=in_)
        self.dcnt[k] += 16
        ins.then_inc(self.dsem[k], 16)
        self._mark((self.dsem[k], self.dcnt[k]), rd, wr)

    def finish(self):
        for k in range(self.NDMA):
            if self.dcnt[k]:
                self.nc.sync.wait_ge(self.dsem[k], self.dcnt[k])
        for e, s in self.psem.items():
            if self.pcnt[e]:
                self.nc.sync.wait_ge(s, self.pcnt[e])


class Ctx:
    pass


def setup_common(kb, nblk_cols=512):
    c = Ctx()
    c.P = [kb.ps("P%d" % i, [128, 512]) for i in range(6)]
    c.ones = kb.sb("ones", [128, 128], F32)
    kb.op("dve", lambda e: e.memset(c.ones.t[:], 1.0 / D), wr=[c.ones.r])
    return c


def alloc_p67(kb, c):
    c.P = c.P[:6] + [kb.ps("P6", [128, 512]), kb.ps("P7", [128, 512])]


def alloc_pt(kb, c):
    c.PT = [kb.ps("PT%d" % i, [128, 8, 128], BF16) for i in range(2)]


def load_mod(kb, c, mod_ap, lnp_ap, layer):
    c.mod = kb.sb("mod%d" % layer, [128, 9, KC, 2], F32)
    c.lnp = kb.sb("lnp%d" % layer, [128, 6, KC], F32)
    c.sc1 = kb.sb("sc1_%d" % layer, [128, 3, KC, 2], F32)
    c.gsc = kb.sb("gsc_%d" % layer, [128, 3, KC, 2], F32)
    kb.dma("sp", c.mod.t[:].rearrange("p a k w -> p (a k w)"), mod_ap[layer], wr=[c.mod.r])
    kb.dma("sp", c.lnp.t[:].rearrange("p a k -> p (a k)"), lnp_ap[layer], wr=[c.lnp.r])
    for s in range(3):
        ms = min(s, 1)
        kb.op("dve", lambda e, s=s: e.tensor_scalar(out=c.sc1.t[:, s], in0=c.mod.t[:, 3 * ms + 1], scalar1=1.0,
                                                     scalar2=None, op0=ALU.add), rd=[c.mod.r], wr=[c.sc1.r])
        f = (0.5 if s != 1 else 1.0) / ALPHA
        kb.op("dve", lambda e, s=s, f=f: e.tensor_scalar(out=c.gsc.t[:, s], in0=c.mod.t[:, 3 * ms + 2], scalar1=f,
                                                          scalar2=None, op0=ALU.mult), rd=[c.mod.r], wr=[c.gsc.r])


def blk_w(c0):
    return 1 if c0 >= TL else 0


def emit_modulate(kb, c, x, h, sub, blks, hoff=0):
    i = 0
    for kc in range(KC):
        for (c0, n) in blks:
            w = blk_w(c0)
            i += 1
            if i % 2:
                kb.op("dve", lambda e, kc=kc, c0=c0, n=n, w=w: e.tensor_scalar(
                    out=h.t[:, kc, c0 - hoff:c0 - hoff + n], in0=x.t[:, kc, c0:c0 + n],
                    scalar1=c.sc1.t[:, sub, kc, w:w + 1], scalar2=c.mod.t[:, 3 * sub, kc, w:w + 1],
                    op0=ALU.mult, op1=ALU.add),
                    rd=[x.rs[kc], c.sc1.r, c.mod.r], wr=[h.rs[kc]])
            else:
                kb.op("act", lambda e, kc=kc, c0=c0, n=n, w=w: e.activation(
                    out=h.t[:, kc, c0 - hoff:c0 - hoff + n], in_=x.t[:, kc, c0:c0 + n], func=AF.Identity,
                    scale=c.sc1.t[:, sub, kc, w:w + 1], bias=c.mod.t[:, 3 * sub, kc, w:w + 1]),
                    rd=[x.rs[kc], c.sc1.r, c.mod.r], wr=[h.rs[kc]])


def emit_ln(kb, c, x, sub, blks, tmp):
    emit_ln_g(kb, c, x, KC, blks, tmp, c.ones, EPS_S,
              lambda kc: c.lnp.t[:, 2 * sub, kc:kc + 1], lambda kc: c.lnp.t[:, 2 * sub + 1, kc:kc + 1], [c.lnp.r])


def emit_ln_g(kb, c, x, nch, blks, tmp, ones, eps, gfn, bfn, prs, out=None, func=None):
    mean_ps, ex2_ps = c.P[6], c.P[7]
    func = func or AF.Identity
    for (c0, n) in blks:
        mean, var, rstd, nmr = tmp["mean"], tmp["var"], tmp["rstd"], tmp["nmr"]
        for kc in range(nch):
            if kc == 0:
                kb.op("act", lambda e: e.activation(out=var.t[:, 0:n], in_=x.t[:, kc, c0:c0 + n], func=AF.Square),
                      rd=[x.rs[kc]], wr=[var.r])
                continue
            sq = tmp["sq"][kc % 2]
            kb.op("act", lambda e: e.activation(out=sq.t[:, 0:n], in_=x.t[:, kc, c0:c0 + n], func=AF.Square),
                  rd=[x.rs[kc]], wr=[sq.r])
            kb.op("dve", lambda e: e.tensor_tensor(out=var.t[:, 0:n], in0=var.t[:, 0:n], in1=sq.t[:, 0:n], op=ALU.add),
                  rd=[var.r, sq.r], wr=[var.r])
            if kc == 1:
                kb.op("pool", lambda e: e.tensor_tensor(out=mean.t[:, 0:n], in0=x.t[:, 0, c0:c0 + n], in1=x.t[:, 1, c0:c0 + n], op=ALU.add),
                      rd=[x.rs[0], x.rs[1]], wr=[mean.r])
            else:
                kb.op("pool", lambda e: e.tensor_tensor(out=mean.t[:, 0:n], in0=mean.t[:, 0:n], in1=x.t[:, kc, c0:c0 + n], op=ALU.add),
                      rd=[mean.r, x.rs[kc]], wr=[mean.r])
        kb.op("pe", lambda e: e.matmul(mean_ps.t[:, 0:n], ones.t[:], mean.t[:, 0:n], start=True, stop=True),
              rd=[mean.r, ones.r], wr=[mean_ps.r])
        kb.op("pe", lambda e: e.matmul(ex2_ps.t[:, 0:n], ones.t[:], var.t[:, 0:n], start=True, stop=True),
              rd=[var.r, ones.r], wr=[ex2_ps.r])
        kb.op("act", lambda e: e.copy(out=mean.t[:, 0:n], in_=mean_ps.t[:, 0:n]), rd=[mean_ps.r], wr=[mean.r])
        kb.op("dve", lambda e: e.tensor_tensor(out=var.t[:, 0:n], in0=mean.t[:, 0:n], in1=mean.t[:, 0:n], op=ALU.mult),
              rd=[mean.r], wr=[var.r])
        kb.op("dve", lambda e: e.tensor_tensor(out=var.t[:, 0:n], in0=ex2_ps.t[:, 0:n], in1=var.t[:, 0:n], op=ALU.subtract),
              rd=[ex2_ps.r, var.r], wr=[var.r])
        kb.op("act", lambda e: e.activation(out=var.t[:, 0:n], in_=var.t[:, 0:n], func=AF.Sqrt, bias=eps, scale=1.0),
              rd=[var.r], wr=[var.r])
        kb.op("dve", lambda e: e.reciprocal(out=rstd.t[:, 0:n], in_=var.t[:, 0:n]), rd=[var.r], wr=[rstd.r])
        kb.op("dve", lambda e: e.scalar_tensor_tensor(out=nmr.t[:, 0:n], in0=mean.t[:, 0:n], scalar=-1.0, in1=rstd.t[:, 0:n],
                                                      op0=ALU.mult, op1=ALU.mult), rd=[mean.r, rstd.r], wr=[nmr.r])
        for kc in range(nch):
            t1 = tmp["t1"][kc % 2]
            kb.op("dve", lambda e: e.tensor_tensor(out=t1.t[:, 0:n], in0=x.t[:, kc, c0:c0 + n], in1=rstd.t[:, 0:n],
                                                   op=ALU.mult), rd=[x.rs[kc], rstd.r], wr=[t1.r])
            kb.op("dve", lambda e: e.tensor_tensor(out=t1.t[:, 0:n], in0=t1.t[:, 0:n], in1=nmr.t[:, 0:n], op=ALU.add),
                  rd=[t1.r, nmr.r], wr=[t1.r])
            o = x if out is None else out
            kb.op("act", lambda e: e.activation(out=o.t[:, kc, c0:c0 + n], in_=t1.t[:, 0:n], func=func,
                                                scale=gfn(kc), bias=bfn(kc)),
                  rd=[t1.r] + prs, wr=[o.rs[kc]])


def make_ln_tmp(kb):
    tmp = {}
    tmp["sq"] = [kb.sb("ln_sq%d" % i, [128, 512], F32) for i in range(2)]
    tmp["t1"] = [kb.sb("ln_t1%d" % i, [128, 512], F32) for i in range(2)]
    for nm in ("mean", "var", "rstd", "nmr"):
        tmp[nm] = kb.sb("ln_" + nm, [128, 512], F32)
    return tmp


def emit_ffn(kb, c, x, sub, w13_ap, w2_ap, passes, bufs):
    h, g, w13, w2t, sa = bufs["h"], bufs["g"], bufs["w13"], bufs["w2"], bufs["sa"]
    P = c.P
    for blks in passes:
        hoff = blks[0][0]
        def hc(c0):
            return (c0 - hoff) if c0 < TL else 512
        i = 0
        for kc in range(KC):
            for (c0, n) in blks:
                w = blk_w(c0)
                i += 1
                if i % 2:
                    kb.op("dve", lambda e, kc=kc, c0=c0, n=n, w=w: e.tensor_scalar(
                        out=h.t[:, kc, hc(c0):hc(c0) + n], in0=x.t[:, kc, c0:c0 + n],
                        scalar1=c.sc1.t[:, sub, kc, w:w + 1], scalar2=c.mod.t[:, 3 * min(sub, 1), kc, w:w + 1],
                        op0=ALU.mult, op1=ALU.add), rd=[x.rs[kc], c.sc1.r, c.mod.r], wr=[h.rs[kc]])
                else:
                    kb.op("act", lambda e, kc=kc, c0=c0, n=n, w=w: e.activation(
                        out=h.t[:, kc, hc(c0):hc(c0) + n], in_=x.t[:, kc, c0:c0 + n], func=AF.Identity,
                        scale=c.sc1.t[:, sub, kc, w:w + 1], bias=c.mod.t[:, 3 * min(sub, 1), kc, w:w + 1]),
                        rd=[x.rs[kc], c.sc1.r, c.mod.r], wr=[h.rs[kc]])
        def load13(jj):
            for s in range(2):
                kb.dma(WQ, w13[s][jj % 2].t[:], w13_ap[s, jj], wr=[w13[s][jj % 2].r])
        load13(0)
        for jj in range(22):
            if jj + 1 < 22:
                load13(jj + 1)
            for jl in range(2):
                j = 2 * jj + jl
                pb = (j % 2) * 3
                for s in range(2):
                    wt = w13[s][jj % 2]
                    for kc in range(KC):
                        for (c0, n) in blks:
                            if c0 < TL:
                                o = P[pb + s].t[:, 0:n]
                                ro = P[pb + s].r
                            else:
                                o = P[pb + 2].t[:, 32 * s:32 * s + n]
                                ro = P[pb + 2].rs[0]
                            kb.op("pe", lambda e, o=o, wt=wt, kc=kc, c0=c0, n=n: e.matmul(
                                o, wt.t[:, kc, 128 * jl:128 * jl + 128], h.t[:, kc, hc(c0):hc(c0) + n],
                                start=(kc == 0), stop=(kc == KC - 1)), rd=[wt.r, h.rs[kc]], wr=[ro])
                for (c0, n) in blks:
                    st = sa[j % 2]
                    if c0 < TL:
                        a_ap, b_ap, ra = P[pb].t[:, 0:n], P[pb + 1].t[:, 0:n], [P[pb].r, P[pb + 1].r]
                        so = st.t[:, 0:n]
                    else:
                        a_ap, b_ap, ra = P[pb + 2].t[:, 0:n], P[pb + 2].t[:, 32:32 + n], [P[pb + 2].r]
                        so = st.t[:, 512:512 + n]
                    kb.op("act", lambda e, so=so, a_ap=a_ap: e.activation(out=so, in_=a_ap, func=AF.Silu), rd=ra, wr=[st.r])
                    kb.op("dve", lambda e, so=so, b_ap=b_ap, c0=c0, n=n, j=j: e.tensor_tensor(
                        out=g.t[:, j, hc(c0):hc(c0) + n], in0=so, in1=b_ap, op=ALU.mult), rd=ra + [st.r], wr=[g.rs[j]])
        def load2(q):
            kb.dma(WQ, w2t[q % 2].t[:], w2_ap[q // 4, q % 4], wr=[w2t[q % 2].r])
        load2(0)
        for dg in range(8):
            pb = (dg % 2) * 3
            for jb in range(4):
                q = dg * 4 + jb
                if q + 1 < 32:
                    load2(q + 1)
                wt = w2t[q % 2]
                for ji in range(11):
                    j = jb * 11 + ji
                    for dl in range(2):
                        for (c0, n) in blks:
                            if c0 < TL:
                                o, ro = P[pb + dl].t[:, 0:n], P[pb + dl].r
                            else:
                                cb = P[pb + 2] if dl == 0 else P[6 + dg % 2]
                                o, ro = cb.t[:, 0:n], cb.r
                            kb.op("pe", lambda e, o=o, wt=wt, ji=ji, dl=dl, j=j, c0=c0, n=n: e.matmul(
                                o, wt.t[:, ji, 128 * dl:128 * dl + 128], g.t[:, j, hc(c0):hc(c0) + n],
                                start=(j == 0), stop=(j == NJ - 1)), rd=[wt.r, g.rs[j]], wr=[ro])
            for dl in range(2):
                d = 2 * dg + dl
                for (c0, n) in blks:
                    w = blk_w(c0)
                    if c0 < TL:
                        y_ap, ry = P[pb + dl].t[:, 0:n], P[pb + dl].r
                    else:
                        cb = P[pb + 2] if dl == 0 else P[6 + dg % 2]
                        y_ap, ry = cb.t[:, 0:n], cb.r
                    kb.op("dve", lambda e, y_ap=y_ap, d=d, c0=c0, n=n, w=w: e.scalar_tensor_tensor(
                        out=x.t[:, d, c0:c0 + n], in0=y_ap, scalar=c.gsc.t[:, sub, d, w:w + 1], in1=x.t[:, d, c0:c0 + n],
                        op0=ALU.mult, op1=ALU.add), rd=[ry, c.gsc.r, x.rs[d]], wr=[x.rs[d]])
        emit_ln(kb, c, x, sub, blks, bufs["ln"])


def make_ffn_bufs(kb):
    b = {}
    b["h"] = kb.sb("ffn_h", [128, KC, 544], BF16, nres=KC)
    b["g"] = kb.sb("ffn_g", [128, NJ, 544], BF16, nres=NJ)
    b["w13"] = [[kb.sb("w13_%d_%d" % (s, i), [128, KC, 256], BF16) for i in range(2)] for s in range(2)]
    b["w2"] = [kb.sb("w2_%d" % i, [128, 11, 256], BF16) for i in range(2)]
    b["sa"] = [kb.sb("sa%d" % i, [128, 544], F32) for i in range(2)]
    b["ln"] = make_ln_tmp(kb)
    return b


FFN_PASSES = [[(0, 512), (TL, TCX)], [(512, 512)]]


def build_mod():
    nc = bass.Bass("TRN2", target_bir_lowering=False)
    NCOL = 2304
    HC = NCOL // 2
    wada = nc.dram_tensor("wada", [DEPTH, 2, 128, KC, HC], F32, kind="ExternalInput").ap()
    bada = nc.dram_tensor("bada", [2, DEPTH, NCOL], F32, kind="ExternalInput").ap()
    cc = nc.dram_tensor("cc", [128, KC, 2], F32, kind="ExternalInput").ap()
    out = nc.dram_tensor("modo", [2, DEPTH, NCOL], F32, kind="ExternalOutput").ap()
    wci = nc.dram_tensor("wci", [128, WC_M], F32, kind="ExternalInput").ap()
    wco = nc.dram_tensor("wco", [128, WC_M], BF16, kind="ExternalOutput").ap()
    with ExitStack() as es:
        kb = KB(nc, es)
        with ExitStack() as es2:
            kb.es = es2
            cb = [kb.sb("cb%d" % i, [128, WC_CH], BF16) for i in range(4)]
            for q in range(WC_M // WC_CH):
                t_ = cb[q % 4]
                kb.dma("pool", t_.t[:], wci[:, q * WC_CH:(q + 1) * WC_CH], wr=[t_.r])
                kb.dma("sp", wco[:, q * WC_CH:(q + 1) * WC_CH], t_.t[:], rd=[t_.r])
            kb.barrier()
            kb.es = es
        ct = kb.sb("ct", [128, KC, 2], F32)
        sg = kb.sb("sg", [128, KC, 2], F32)
        wt = [kb.sb("wt%d" % i, [128, KC, HC], F32) for i in range(2)]
        bt = kb.sb("bt", [2, DEPTH, NCOL], F32)
        ot = kb.sb("ot", [2, DEPTH, NCOL], F32)
        ps = [kb.ps("ps%d" % i, [128, 512]) for i in range(6)]
        kb.dma("sp", ct.t[:], cc, wr=[ct.r])
        kb.dma("sp", bt.t[:], bada, wr=[bt.r])
        kb.op("act", lambda e: e.activation(out=sg.t[:], in_=ct.t[:], func=AF.Silu), rd=[ct.r], wr=[sg.r])
        q = 0
        for l in range(DEPTH):
            for hh in range(2):
                w_ = wt[q % 2]
                kb.dma("sp", w_.t[:], wada[l, hh], wr=[w_.r])
                for bi, (n0, n) in enumerate([(0, 512), (512, 512), (1024, HC - 1024)]):
                    pt = ps[(q % 2) * 3 + bi]
                    for kc in range(KC):
                        kb.op("pe", lambda e: e.matmul(pt.t[0:2, 0:n], sg.t[:, kc, :], w_.t[:, kc, n0:n0 + n],
                                                       start=(kc == 0), stop=(kc == KC - 1)), rd=[w_.r, sg.r], wr=[pt.r])
                    c0 = hh * HC + n0
                    kb.op("dve", lambda e: e.tensor_tensor(out=ot.t[:, l, c0:c0 + n], in0=pt.t[0:2, 0:n], in1=bt.t[:, l, c0:c0 + n],
                                                           op=ALU.add), rd=[pt.r, bt.r], wr=[ot.r])
                q += 1
        kb.dma("sp", out, ot.t[:], rd=[ot.r])
        kb.finish()
    return nc


WC_NAMES = [("f1w13", (2, 22, 128, KC, 256)), ("f1w2", (8, 4, 128, 11, 256)), ("f2w13", (2, 22, 128, KC, 256)),
            ("f2w2", (8, 4, 128, 11, 256)), ("winf", (14, 128, KC, 256)), ("winv", (2, 128, KC, 512)),
            ("wgb", (16, 128, 64, 128)), ("wo", (16, 128, KC, 128))]
WC_SIZES = [int(np.prod(sh)) // (NCORE * 128) for _, sh in WC_NAMES]
WC_M = DEPTH * sum(WC_SIZES)
WC_CH = 4864
assert WC_M % WC_CH == 0


def tiled_weights(inp, l):
    winf, winv = tile_win(inp["w_in"][l])
    tg = inp["w_gate"][l].reshape(KC, 128, 3, 16, 128).transpose(3, 1, 2, 0, 4).reshape(16, 128, 48, 128)
    tbn = inp["w_b_na"][l].reshape(8, 128, 16, 128).transpose(2, 1, 0, 3)
    tbf = inp["w_b_f"][l].reshape(4, 128, 16, 128).transpose(2, 1, 0, 3)
    tbc = inp["w_b_c"][l].reshape(4, 128, 16, 128).transpose(2, 1, 0, 3)
    return {"f1w13": tile_w13(inp["ffn1_w1"][l], inp["ffn1_w3"][l]), "f1w2": tile_w2(inp["ffn1_w2"][l]),
            "f2w13": tile_w13(inp["ffn2_w1"][l], inp["ffn2_w3"][l]), "f2w2": tile_w2(inp["ffn2_w2"][l]),
            "winf": winf, "winv": winv, "wgb": np.ascontiguousarray(np.concatenate([tg, tbn, tbf, tbc], axis=2)),
            "wo": np.ascontiguousarray(inp["w_o"][l].reshape(KC, 128, 16, 128).transpose(2, 1, 0, 3))}


def run_mod(c, c_ctx, w_ada, b_ada, inp):
    tw = [tiled_weights(inp, l) for l in range(DEPTH)]
    slabs = [[] for _ in range(NCORE)]
    for l in range(DEPTH):
        for (nm, sh), m in zip(WC_NAMES, WC_SIZES):
            a = tw[l][nm].reshape(NCORE, 128, m)
            for i in range(NCORE):
                slabs[i].append(a[i])
    del tw
    cc = np.stack([c.reshape(KC, 128).T, c_ctx.reshape(KC, 128).T], axis=-1)
    cc = np.ascontiguousarray(cc, dtype=np.float32)
    in_maps = []
    for i in range(NCORE):
        ws = w_ada[:, :, 2304 * i:2304 * (i + 1)]
        ws = np.ascontiguousarray(ws.reshape(DEPTH, KC, 128, 2, 1152).transpose(0, 3, 2, 1, 4))
        bs = np.ascontiguousarray(np.broadcast_to(b_ada[None, :, 2304 * i:2304 * (i + 1)], (2, DEPTH, 2304)))
        in_maps.append({"wada": ws, "bada": bs, "cc": cc, "wci": np.ascontiguousarray(np.concatenate(slabs[i], axis=1))})
    del slabs
    res = run_bass_kernel_spmd(build_mod(), in_maps, core_ids=list(range(NCORE)))
    del in_maps
    wco = [np.asarray(res.results[i]["wco"]) for i in range(NCORE)]
    wb = []
    off = 0
    for l in range(DEPTH):
        d = {}
        for (nm, sh), m in zip(WC_NAMES, WC_SIZES):
            d[nm] = np.ascontiguousarray(np.stack([wco[i][:, off:off + m] for i in range(NCORE)], axis=0)).reshape(sh)
            off += m
        wb.append(d)
    mod = np.zeros((DEPTH, 2, 9 * D), np.float32)
    for i in range(NCORE):
        o = np.asarray(res.results[i]["modo"])
        mod[:, :, 2304 * i:2304 * (i + 1)] = o.transpose(1, 0, 2)
    modT = mod.reshape(DEPTH, 2, 9, KC, 128).transpose(0, 4, 2, 3, 1).reshape(DEPTH, 128, 9 * KC * 2)
    return np.ascontiguousarray(modT), wb


TOK_TILES = [(128 * i, 128) for i in range(8)] + [(TL, TCX)]
ALL_BLKS = [(0, 512), (512, 512), (TL, TCX)]


def emit_proj(kb, c, x, winf_ap, winv_ap, cs128_ap, outs, kv_only=False):
    P = c.P
    with ExitStack() as es2:
        old = kb.es
        kb.es = es2
        h2 = kb.sb("pj_h2", [128, KC, NT], BF16, nres=KC)
        wf = [kb.sb("pj_wf%d" % i, [128, KC, 256], BF16) for i in range(2)]
        wv = [kb.sb("pj_wv%d" % i, [128, KC, 512], BF16) for i in range(2)]
        stg = [kb.sb("pj_stg%d" % i, [128, NT], BF16) for i in range(2)]
        ustg = [kb.sb("pj_us%d" % i, [128, NT], F32) for i in range(2)]
        sgt = [kb.sb("pj_sg%d" % i, [128, NT], F32) for i in range(2)]
        fT = kb.sb("pj_fT", [128, 4, NT], BF16, nres=4)
        cs = kb.sb("pj_cs", [128, 256], BF16)
        vst = [kb.sb("pj_vst%d" % i, [128, 1024], BF16) for i in range(2)]
        kb.dma("pool", cs.t[:], cs128_ap, wr=[cs.r])
        emit_modulate(kb, c, x, h2, 1, ALL_BLKS)
        for hh in range(2):
            kb.dma("pool", wv[hh].t[:], winv_ap[hh], wr=[wv[hh].r])
        for ti, (t0, tn) in enumerate(TOK_TILES):
            if kv_only and t0 < TL:
                continue
            vs = vst[ti % 2]
            for hh in range(2):
                pt = P[(ti % 2) * 2 + hh]
                for kc in range(KC):
                    kb.op("pe", lambda e, kc=kc, pt=pt, hh=hh: e.matmul(pt.t[0:tn, :], h2.t[:, kc, t0:t0 + tn], wv[hh].t[:, kc, :],
                                                                       start=(kc == 0), stop=(kc == KC - 1)),
                          rd=[h2.rs[kc], wv[hh].r], wr=[pt.r])
                eng = ("act", "dve")[hh]
                if eng == "act":
                    kb.op("act", lambda e, pt=pt, hh=hh: e.copy(out=vs.t[0:tn, 512 * hh:512 * hh + 512], in_=pt.t[0:tn, :]),
                          rd=[pt.r], wr=[vs.r])
                else:
                    kb.op("dve", lambda e, pt=pt, hh=hh: e.tensor_copy(out=vs.t[0:tn, 512 * hh:512 * hh + 512], in_=pt.t[0:tn, :]),
                          rd=[pt.r], wr=[vs.r])
            kb.dma("sp", outs["V"][t0:t0 + tn, :], vs.t[0:tn, :], rd=[vs.r])
        blks = [(TL, TCX)] if kv_only else ALL_BLKS
        pairs = list(range(4, 8)) if kv_only else list(range(14))
        kb.dma("pool", wf[pairs[0] % 2].t[:], winf_ap[pairs[0]], wr=[wf[pairs[0] % 2].r])
        for pi, pr in enumerate(pairs):
            if pi + 1 < len(pairs):
                nx = pairs[pi + 1]
                kb.dma("pool", wf[nx % 2].t[:], winf_ap[nx], wr=[wf[nx % 2].r])
            wt = wf[pr % 2]
            for cl in range(2):
                pb = 3 * cl
                for kc in range(KC):
                    for (c0, n) in blks:
                        pt = P[pb + (0 if c0 == 0 else 1 if c0 == 512 else 2)]
                        kb.op("pe", lambda e, pt=pt, kc=kc, c0=c0, n=n, cl=cl: e.matmul(
                            pt.t[:, 0:n], wt.t[:, kc, 128 * cl:128 * cl + 128], h2.t[:, kc, c0:c0 + n],
                            start=(kc == 0), stop=(kc == KC - 1)), rd=[wt.r, h2.rs[kc]], wr=[pt.r])
                if pr < 10:
                    ch = 2 * pr + cl
                    st = stg[ch % 2]
                    for bi, (c0, n) in enumerate(blks):
                        pt = P[pb + (0 if c0 == 0 else 1 if c0 == 512 else 2)]
                        dst = fT.t[:, ch - 16, c0:c0 + n] if ch >= 16 else st.t[:, c0:c0 + n]
                        wr = [fT.rs[ch - 16]] if ch >= 16 else [st.r]
                        if bi % 2 == 0:
                            kb.op("act", lambda e, pt=pt, dst=dst, n=n: e.copy(out=dst, in_=pt.t[:, 0:n]), rd=[pt.r], wr=wr)
                        else:
                            kb.op("dve", lambda e, pt=pt, dst=dst, n=n: e.tensor_copy(out=dst, in_=pt.t[:, 0:n]), rd=[pt.r], wr=wr)
                    if ch < 16:
                        dram = outs["qT"] if ch < 8 else outs["kT"]
                        if kv_only:
                            kb.dma("sp", dram[:, ch % 8, TL:NT], st.t[:, TL:NT], rd=[st.r])
                        else:
                            kb.dma("sp", dram[:, ch % 8, :], st.t[:, :], rd=[st.r])
            if pr >= 10:
                i = pr - 10
                sg, us = sgt[i % 2], ustg[i % 2]
                for (c0, n) in blks:
                    k = 0 if c0 == 0 else 1 if c0 == 512 else 2
                    kb.op("act", lambda e, k=k, c0=c0, n=n: e.activation(out=sg.t[:, c0:c0 + n], in_=P[3 + k].t[:, 0:n], func=AF.Sigmoid),
                          rd=[P[3 + k].r], wr=[sg.r])
                    kb.op("dve", lambda e, k=k, c0=c0, n=n: e.tensor_tensor(out=us.t[:, c0:c0 + n], in0=sg.t[:, c0:c0 + n],
                                                                            in1=P[k].t[:, 0:n], op=ALU.mult),
                          rd=[P[k].r, P[3 + k].r, sg.r], wr=[us.r])
                kb.dma("sp", outs["uT"][:, i, :], us.t[:, :], rd=[us.r])
        if not kv_only:
            for ti, (t0, tn) in enumerate(TOK_TILES):
                vs = vst[ti % 2]
                for gi in range(4):
                    for w in range(2):
                        kb.op("pe", lambda e, gi=gi, w=w: e.matmul(P[6 + w].t[0:tn, 128 * gi:128 * gi + 128], fT.t[:, gi, t0:t0 + tn],
                                                                   cs.t[:, 128 * w:128 * w + 128], start=True, stop=True),
                              rd=[fT.rs[gi], cs.r], wr=[P[6 + w].r])
                kb.op("act", lambda e: e.copy(out=vs.t[0:tn, 0:512], in_=P[6].t[0:tn, :]), rd=[P[6].r], wr=[vs.r])
                kb.op("dve", lambda e: e.tensor_copy(out=vs.t[0:tn, 512:1024], in_=P[7].t[0:tn, :]), rd=[P[7].r], wr=[vs.r])
                kb.dma("sp", outs["XcXs"][t0:t0 + tn, :], vs.t[0:tn, :], rd=[vs.r])
        kb.barrier()
        kb.es = old


def _barrier(self):
    for e, eng in self.eng.items():
        seen = self.seen[e]
        for e2, s in self.psem.items():
            if e2 != e and self.pcnt[e2] and seen.get(s, 0) < self.pcnt[e2]:
                eng.wait_ge(s, self.pcnt[e2])
                seen[s] = self.pcnt[e2]
        for k in range(self.NDMA):
            if self.dcnt[k] and seen.get(self.dsem[k], 0) < self.dcnt[k]:
                eng.wait_ge(self.dsem[k], self.dcnt[k])
                seen[self.dsem[k]] = self.dcnt[k]


KB.barrier = _barrier


def dram_in(nc, name, shape, dt=F32):
    return nc.dram_tensor(name, list(shape), dt, kind="ExternalInput").ap()


def dram_out(nc, name, shape, dt=F32):
    return nc.dram_tensor(name, list(shape), dt, kind="ExternalOutput").ap()


def decl_proj_outs(nc):
    return {"qT": dram_out(nc, "o_qT", [128, 8, NT], BF16), "kT": dram_out(nc, "o_kT", [128, 8, NT], BF16),
            "V": dram_out(nc, "o_V", [NT, 1024], BF16), "XcXs": dram_out(nc, "o_XcXs", [NT, 1024], BF16),
            "uT": dram_out(nc, "o_uT", [128, 4, NT], F32)}


def build_stage1():
    nc = bass.Bass("TRN2", target_bir_lowering=False)
    xT = dram_in(nc, "xT", [128, KC, NT])
    mod = dram_in(nc, "mod", [DEPTH, 128, 9 * KC * 2])
    lnp = dram_in(nc, "lnp", [DEPTH, 128, 6 * KC])
    f1w13 = dram_in(nc, "f1w13", [2, 22, 128, KC, 256], BF16)
    f1w2 = dram_in(nc, "f1w2", [8, 4, 128, 11, 256], BF16)
    winf = dram_in(nc, "winf", [14, 128, KC, 256], BF16)
    winv = dram_in(nc, "winv", [2, 128, KC, 512], BF16)
    cs128 = dram_in(nc, "cs128", [128, 256])
    x1T = dram_out(nc, "x1T", [128, KC, NT])
    outs = decl_proj_outs(nc)
    with ExitStack() as es:
        kb = KB(nc, es)
        c = setup_common(kb)
        alloc_p67(kb, c)
        x = kb.sb("x", [128, KC, NT], F32, nres=KC)
        kb.dma("sp", x.t[:], xT, wr=x.rs)
        load_mod(kb, c, mod, lnp, 0)
        with ExitStack() as es2:
            kb.es = es2
            bufs = make_ffn_bufs(kb)
            emit_ffn(kb, c, x, 0, f1w13, f1w2, FFN_PASSES, bufs)
            kb.barrier()
            kb.es = es
        kb.dma("sp", x1T, x.t[:], rd=x.rs)
        emit_proj(kb, c, x, winf, winv, cs128, outs)
        kb.finish()
    return nc


def tile_w13(w1, w3):
    def t(w):
        return w.reshape(KC, 128, 22, 256).transpose(2, 1, 0, 3)
    return np.ascontiguousarray(np.stack([t(w1), t(w3)], axis=0))


def tile_w2(w2):
    return np.ascontiguousarray(w2.reshape(4, 11, 128, 8, 256).transpose(3, 0, 2, 1, 4))


def tile_win(w_in):
    q = w_in[:, 0:1024]
    k = w_in[:, 1024:2048]
    v = w_in[:, 2048:3072]
    f = w_in[:, 3072:3584]
    a = w_in[:, 3584:4096]
    g = w_in[:, 4096:4608]
    ag = np.concatenate([np.concatenate([a[:, 128 * i:128 * i + 128], g[:, 128 * i:128 * i + 128]], axis=1) for i in range(4)], axis=1)
    fm = np.concatenate([q, k, f, ag], axis=1)
    winf = np.ascontiguousarray(fm.reshape(KC, 128, 14, 256).transpose(2, 1, 0, 3))
    winv = np.ascontiguousarray(v.reshape(KC, 128, 2, 512).transpose(2, 1, 0, 3))
    return winf, winv


def dft_cs128():
    k = np.arange(128)
    ang = 2.0 * np.pi * np.outer(k, k) / 128.0
    return np.ascontiguousarray(np.concatenate([np.cos(ang), np.sin(ang)], axis=1) / np.sqrt(128.0), dtype=np.float32)


def to_fm(a):
    return a.T.reshape(KC, 128, a.shape[0]).transpose(1, 0, 2)


def make_lnp(ln_g, ln_b):
    L = ln_g.shape[0]
    st = np.stack([ln_g[:, 0], ln_b[:, 0], ln_g[:, 1], ln_b[:, 1], ln_g[:, 2], ln_b[:, 2]], axis=1)
    return np.ascontiguousarray(st.reshape(L, 6, KC, 128).transpose(0, 3, 1, 2).reshape(L, 128, 6 * KC))


ATT_WIN = [(0, 12, 0), (2, 10, 1024), (4, 9, 1920), (6, 9, 1920), (8, 9, 1920), (10, 9, 1920), (12, 9, 2752), (12, 11, 3584)]
NBIAS = 4544
NKH = 1536 + CTX
SCALE = 128 ** -0.5


def emit_attn(kb, c, aps, ynaT, with_ctx_q):
    P = c.P
    PT = c.PT
    q_t = [kb.sb("at_q%d" % i, [128, NT], BF16) for i in range(2)]
    k_t = [kb.sb("at_k%d" % i, [128, NKH], BF16) for i in range(2)]
    v_t = [kb.sb("at_v%d" % i, [128, 14, 128], BF16) for i in range(2)]
    b_t = [kb.sb("at_b%d" % i, [128, NBIAS], F32) for i in range(2)]
    s_t = [kb.sb("at_s%d" % i, [128, 1024], F32) for i in range(2)]
    p_t = [kb.sb("at_p%d" % i, [128, 1024], BF16) for i in range(2)]
    pT_t = [kb.sb("at_pT%d" % i, [128, 8, 128], BF16) for i in range(2)]
    on_t = [kb.sb("at_on%d" % i, [128, 128], BF16) for i in range(2)]
    st_t = [kb.sb("at_st%d" % i, [128, 4], F32) for i in range(2)]
    ident = kb.sb("at_id", [128, 128], BF16)
    idf = kb.sb("at_idf", [128, 128], F32)
    kb.dma("sp", idf.t[:], aps["ident"], wr=[idf.r])
    kb.op("dve", lambda e: e.tensor_copy(out=ident.t[:], in_=idf.t[:]), rd=[idf.r], wr=[ident.r])

    def load(h):
        b = h % 2
        kb.dma("sp", q_t[b].t[:], aps["qT"][:, h, :], wr=[q_t[b].r])
        kb.dma("sp", k_t[b].t[:], aps["kTh"][:, h, :], wr=[k_t[b].r])
        kb.dma("sp", v_t[b].t[:], aps["Vh"][h], wr=[v_t[b].r])
        kb.dma("sp", b_t[b].t[:], aps["bias"][h], wr=[b_t[b].r])
    load(0)

    def geom(lt):
        if lt < 8:
            blo, nrow, boff = ATT_WIN[lt]
            return blo, 64 * nrow, boff, 128, 128 * lt
        return 0, 0, 0, TCX, TL

    def stage_a(u, h, lt):
        b0 = 3 * (u % 2)
        qh, kh, bh = q_t[h % 2], k_t[h % 2], b_t[h % 2]
        s, p, stt = s_t[u % 2], p_t[u % 2], st_t[u % 2]
        blo, nk, boff, nq, q0 = geom(lt)
        ntot = nk + CTX
        n1 = nk - 512

        def a1():
            if lt < 8:
                kb.op("pe", lambda e: e.matmul(P[b0].t[:, 0:512], qh.t[:, q0:q0 + 128], kh.t[:, 64 * blo:64 * blo + 512], start=True, stop=True),
                      rd=[qh.r, kh.r], wr=[P[b0].r])
                kb.op("pe", lambda e: e.matmul(P[b0 + 1].t[:, 0:n1], qh.t[:, q0:q0 + 128],
                                               kh.t[:, 64 * blo + 512:64 * blo + nk], start=True, stop=True),
                      rd=[qh.r, kh.r], wr=[P[b0 + 1].r])
                kb.op("pe", lambda e: e.matmul(P[b0 + 1].t[:, n1:n1 + CTX], qh.t[:, q0:q0 + 128], kh.t[:, 1536:NKH], start=True, stop=True),
                      rd=[qh.r, kh.r], wr=[P[b0 + 1].r])
            else:
                kb.op("pe", lambda e: e.matmul(P[b0 + 1].t[0:nq, 0:CTX], qh.t[:, q0:q0 + nq], kh.t[:, 1536:NKH], start=True, stop=True),
                      rd=[qh.r, kh.r], wr=[P[b0 + 1].r])

        def a2():
            if lt < 8:
                kb.op("dve", lambda e: e.scalar_tensor_tensor(out=s.t[:, 0:512], in0=P[b0].t[:, 0:512], scalar=SCALE,
                                                              in1=bh.t[:, boff:boff + 512], op0=ALU.mult, op1=ALU.add),
                      rd=[P[b0].r, bh.r], wr=[s.r])
                kb.op("dve", lambda e: e.scalar_tensor_tensor(out=s.t[:, 512:ntot], in0=P[b0 + 1].t[:, 0:ntot - 512], scalar=SCALE,
                                                              in1=bh.t[:, boff + 512:boff + ntot], op0=ALU.mult, op1=ALU.add),
                      rd=[P[b0 + 1].r, bh.r], wr=[s.r])
            else:
                kb.op("act", lambda e: e.activation(out=s.t[0:nq, 0:CTX], in_=P[b0 + 1].t[0:nq, 0:CTX], func=AF.Copy, scale=SCALE),
                      rd=[P[b0 + 1].r], wr=[s.r])

        def a3():
            kb.op("dve", lambda e: e.reduce_max(out=stt.t[0:nq, 1:2], in_=s.t[0:nq, 0:ntot], axis=mybir.AxisListType.X, negate=True),
                  rd=[s.r], wr=[stt.r])

        def a4():
            kb.op("act", lambda e: e.activation(out=p.t[0:nq, 0:ntot], in_=s.t[0:nq, 0:ntot], func=AF.Exp, bias=stt.t[0:nq, 1:2], scale=1.0,
                                                accum_out=stt.t[0:nq, 2:3]),
                  rd=[s.r, stt.r], wr=[p.r, stt.r])

        def a5():
            kb.op("dve", lambda e: e.reciprocal(out=stt.t[0:nq, 3:4], in_=stt.t[0:nq, 2:3]), rd=[stt.r], wr=[stt.r])
        return [a1, a2, a3, a4, a5]

    def stage_b(u, h, lt):
        b0 = 3 * (u % 2)
        vh = v_t[h % 2]
        p, pT, on, stt = p_t[u % 2], pT_t[u % 2], on_t[u % 2], st_t[u % 2]
        ptb = PT[u % 2]
        blo, nk, boff, nq, q0 = geom(lt)
        chunks = []
        for ci in range((nk + 127) // 128):
            chunks.append((128 * ci, min(128, nk - 128 * ci), blo // 2 + ci))
        chunks += [(nk, 128, 12), (nk + 128, 128, 13)]
        nch = len(chunks)
        o_ps = P[b0 + 2]

        def b1():
            for ci, (c0, w, vc) in enumerate(chunks):
                kb.op("pe", lambda e: e.transpose(ptb.t[0:w, ci, 0:nq], p.t[0:nq, c0:c0 + w], ident.t[0:nq, 0:nq]),
                      rd=[p.r, ident.r], wr=[ptb.r])

        def b2():
            kb.op("dve", lambda e: e.tensor_copy(out=pT.t[:, 0:nch, 0:nq], in_=ptb.t[:, 0:nch, 0:nq]), rd=[ptb.r], wr=[pT.r])

        def b3():
            for ci, (c0, w, vc) in enumerate(chunks):
                kb.op("pe", lambda e: e.matmul(o_ps.t[0:nq, 0:128], pT.t[0:w, ci, 0:nq], vh.t[0:w, vc, :],
                                               start=(ci == 0), stop=(ci == nch - 1)), rd=[pT.r, vh.r], wr=[o_ps.r])

        def b4():
            kb.op("dve", lambda e: e.tensor_scalar(out=on.t[0:nq, :], in0=o_ps.t[0:nq, 0:128], scalar1=stt.t[0:nq, 3:4], scalar2=None,
                                                   op0=ALU.mult), rd=[o_ps.r, stt.r], wr=[on.r])

        def b5():
            kb.op("pe", lambda e: e.transpose(ptb.t[:, 0, 0:nq], on.t[0:nq, :], ident.t[0:nq, 0:nq]), rd=[on.r, ident.r], wr=[ptb.r])

        def b6():
            kb.op("act", lambda e: e.copy(out=ynaT.t[:, h, q0:q0 + nq], in_=ptb.t[:, 0, 0:nq]), rd=[ptb.r], wr=[ynaT.rs[h]])
        return [b1, b2, b3, b4, b5, b6]

    tiles = list(range(8)) + ([8] if with_ctx_q else [])
    units = [(h, lt) for h in range(8) for lt in tiles]
    prev = None
    for u, (h, lt) in enumerate(units):
        a = stage_a(u, h, lt)
        b = stage_b(*prev) if prev is not None else [lambda: None] * 6
        a[0](); b[0](); a[1](); b[1](); a[2](); b[2](); a[3](); b[3](); b[4](); a[4](); b[5]()
        if lt == 0 and h + 1 < 8:
            load(h + 1)
        prev = (u, h, lt)
    for f in stage_b(*prev):
        f()


def emit_fft(kb, c, aps, yfT, with_ctx):
    P = c.P
    NB = 3
    xt = [kb.sb("ff_x%d" % i, [128, 1024], BF16) for i in range(NB)]
    xr = [kb.sb("ff_r%d" % i, [128, 1024], BF16) for i in range(NB)]
    eo = [kb.sb("ff_e%d" % i, [128, 1024], BF16) for i in range(NB)]
    cn = [kb.sb("ff_c%d" % i, [128, 2, 1024], BF16) for i in range(NB)]
    cnc = kb.sb("ff_cc", [128, 2, TCX], BF16)
    xn = kb.sb("ff_xn", [1, 1024], BF16)
    alt = kb.sb("ff_alt", [1, 1024], BF16)
    kb.dma("sp", xn.t[:], aps["xnyq"], wr=[xn.r])
    kb.dma("sp", alt.t[:], aps["alt"], wr=[alt.r])
    NTC = SEQ // 2 // 128

    def load(tc):
        b = tc % NB
        kb.dma("sp", xt[b].t[:], aps["Xall"][128 * tc:128 * tc + 128, :], wr=[xt[b].r])
        kb.dma("sp", xr[b].t[:], aps["Xrev"][128 * tc:128 * tc + 128, :], wr=[xr[b].r])
        kb.dma("act", cn[b].t[:], aps["cn"][tc], wr=[cn[b].r])
    load(0)
    load(1)
    for tc in range(NTC):
        if tc + 2 < NTC:
            load(tc + 2)
        b = tc % NB
        x_, r_, e_, c_ = xt[b], xr[b], eo[b], cn[b]
        kb.op("pool", lambda e: e.tensor_tensor(out=e_.t[:, 0:512], in0=x_.t[:, 0:512], in1=r_.t[:, 0:512], op=ALU.add),
              rd=[x_.r, r_.r], wr=[e_.r])
        kb.op("pool", lambda e: e.tensor_tensor(out=e_.t[:, 512:1024], in0=x_.t[:, 512:1024], in1=r_.t[:, 512:1024], op=ALU.subtract),
              rd=[x_.r, r_.r], wr=[e_.r])
        for m in range(4):
            for kh in range(2):
                acc = P[2 * m + kh]
                for w in range(2):
                    kb.op("pe", lambda e: e.matmul(acc.t[:, :], e_.t[:, 512 * w + 128 * m:512 * w + 128 * m + 128],
                                                   c_.t[:, w, 512 * kh:512 * kh + 512],
                                                   start=(tc == 0 and w == 0), stop=False),
                          rd=[e_.r, c_.r], wr=[acc.r])
    for m in range(4):
        for kh in range(2):
            acc = P[2 * m + kh]
            kb.op("pe", lambda e: e.matmul(acc.t[:, :], xn.t[0:1, 128 * m:128 * m + 128], alt.t[0:1, 512 * kh:512 * kh + 512],
                                           start=False, stop=True), rd=[xn.r, alt.r], wr=[acc.r])
    for m in range(4):
        for kh in range(2):
            acc = P[2 * m + kh]
            if kh == 0:
                kb.op("act", lambda e: e.copy(out=yfT.t[:, m, 512 * kh:512 * kh + 512], in_=acc.t[:, :]), rd=[acc.r], wr=[yfT.rs[m]])
            else:
                kb.op("dve", lambda e: e.tensor_copy(out=yfT.t[:, m, 512 * kh:512 * kh + 512], in_=acc.t[:, :]), rd=[acc.r], wr=[yfT.rs[m]])
    if with_ctx:
        for tc in range(2):
            x_ = xt[tc]
            kb.dma("sp", x_.t[:], aps["Xctx"][128 * tc:128 * tc + 128, :], wr=[x_.r])
            kb.dma("sp", cnc.t[:], aps["cnc"][tc], wr=[cnc.r])
            for m in range(4):
                for w in range(2):
                    kb.op("pe", lambda e: e.matmul(P[m].t[:, 0:TCX], x_.t[:, 512 * w + 128 * m:512 * w + 128 * m + 128], cnc.t[:, w, :],
                                                   start=(tc == 0 and w == 0), stop=(tc == 1 and w == 1)),
                          rd=[x_.r, cnc.r], wr=[P[m].r])
        for m in range(4):
            kb.op("act", lambda e: e.copy(out=yfT.t[:, m, TL:NT], in_=P[m].t[:, 0:TCX]), rd=[P[m].r], wr=[yfT.rs[m]])


def emit_conv(kb, c, aps, ycT, with_ctx, lntmp):
    ut = kb.sb("cv_u", [128, 4, TL + 30], F32)
    uc = kb.sb("cv_uc", [128, 4, TCX + 30], F32)
    acc = kb.sb("cv_acc", [128, 4, NT], F32, nres=4)
    wd = kb.sb("cv_w", [128, 4, 31], F32)
    cp = kb.sb("cv_p", [128, 4, 3], F32)
    ones4 = kb.sb("cv_ones", [128, 128], F32)
    kb.op("dve", lambda e: e.memset(ones4.t[:], 1.0 / 512.0), wr=[ones4.r])
    kb.dma("sp", ut.t[:], aps["uTh"], wr=[ut.r])
    kb.dma("sp", uc.t[:], aps["uTc"], wr=[uc.r])
    kb.dma("sp", wd.t[:], aps["wdw"], wr=[wd.r])
    kb.dma("sp", cp.t[:], aps["cvp"], wr=[cp.r])
    srcs = [(ut, 0, TL)] + ([(uc, TL, TCX)] if with_ctx else [])
    for ci in range(4):
        eng = "dve"
        for (src, c0, n) in srcs:
            kb.op(eng, lambda e: e.tensor_scalar(out=acc.t[:, ci, c0:c0 + n], in0=src.t[:, ci, 0:n], scalar1=wd.t[:, ci, 0:1],
                                                 scalar2=cp.t[:, ci, 0:1], op0=ALU.mult, op1=ALU.add),
                  rd=[src.r, wd.r, cp.r], wr=[acc.rs[ci]])
            for j in range(1, 31):
                kb.op(eng, lambda e: e.scalar_tensor_tensor(out=acc.t[:, ci, c0:c0 + n], in0=src.t[:, ci, j:j + n], scalar=wd.t[:, ci, j:j + 1],
                                                            in1=acc.t[:, ci, c0:c0 + n], op0=ALU.mult, op1=ALU.add),
                      rd=[src.r, wd.r, acc.rs[ci]], wr=[acc.rs[ci]])
    def finish_ln():
        blks = ALL_BLKS if with_ctx else ALL_BLKS[:2]
        emit_ln_g(kb, c, acc, 4, blks, lntmp, ones4, LN_EPS, lambda kc: cp.t[:, kc, 1:2], lambda kc: cp.t[:, kc, 2:3], [cp.r],
                  out=ycT, func=AF.Silu)
    return finish_ln


def emit_merge(kb, c, x, ynaT, yfT, ycT, wgb_ap, wo_ap, bg_ap, with_ctx, lntmp):
    P = c.P
    h2 = kb.sb("mg_h2", [128, KC, 512], BF16, nres=KC)
    mT = kb.sb("mg_m", [128, KC, 512], BF16, nres=KC)
    wgb = [kb.sb("mg_w%d" % i, [128, 64, 128], BF16) for i in range(2)]
    wo = [kb.sb("mg_wo%d" % i, [128, KC, 128], BF16) for i in range(2)]
    gt = [kb.sb("mg_g%d" % i, [128, 512], F32) for i in range(3)]
    mt = [kb.sb("mg_t%d" % i, [128, 512], F32) for i in range(3)]
    bg = kb.sb("mg_bg", [128, 3, KC], F32)
    kb.dma("sp", bg.t[:], bg_ap, wr=[bg.r])
    blks = ALL_BLKS if with_ctx else ALL_BLKS[:2]
    ys = [(ynaT, 8, 48), (yfT, 4, 56), (ycT, 4, 60)]
    for (c0, n) in blks:
        w = blk_w(c0)
        for kc in range(KC):
            if kc % 2:
                kb.op("dve", lambda e: e.tensor_scalar(out=h2.t[:, kc, 0:n], in0=x.t[:, kc, c0:c0 + n], scalar1=c.sc1.t[:, 1, kc, w:w + 1],
                                                       scalar2=c.mod.t[:, 3, kc, w:w + 1], op0=ALU.mult, op1=ALU.add),
                      rd=[x.rs[kc], c.sc1.r, c.mod.r], wr=[h2.rs[kc]])
            else:
                kb.op("act", lambda e: e.activation(out=h2.t[:, kc, 0:n], in_=x.t[:, kc, c0:c0 + n], func=AF.Identity,
                                                    scale=c.sc1.t[:, 1, kc, w:w + 1], bias=c.mod.t[:, 3, kc, w:w + 1]),
                      rd=[x.rs[kc], c.sc1.r, c.mod.r], wr=[h2.rs[kc]])
        kb.dma("pool", wgb[0].t[:], wgb_ap[0], wr=[wgb[0].r])
        for d in range(16):
            if d + 1 < 16:
                kb.dma("pool", wgb[(d + 1) % 2].t[:], wgb_ap[d + 1], wr=[wgb[(d + 1) % 2].r])
            wt = wgb[d % 2]
            for br in range(3):
                for kc in range(KC):
                    kb.op("pe", lambda e: e.matmul(P[br].t[:, 0:n], wt.t[:, br * KC + kc, :], h2.t[:, kc, 0:n],
                                                   start=(kc == 0), stop=(kc == KC - 1)), rd=[wt.r, h2.rs[kc]], wr=[P[br].r])
            for br, (yt, nk, off) in enumerate(ys):
                for k in range(nk):
                    kb.op("pe", lambda e: e.matmul(P[3 + br].t[:, 0:n], wt.t[:, off + k, :], yt.t[:, k, c0:c0 + n],
                                                   start=(k == 0), stop=(k == nk - 1)), rd=[wt.r, yt.rs[k]], wr=[P[3 + br].r])
            for br in range(3):
                kb.op("act", lambda e: e.activation(out=gt[br].t[:, 0:n], in_=P[br].t[:, 0:n], func=AF.Sigmoid,
                                                    bias=bg.t[:, br, d:d + 1], scale=1.0), rd=[P[br].r, bg.r], wr=[gt[br].r])
                kb.op("dve", lambda e: e.tensor_tensor(out=mt[br].t[:, 0:n], in0=gt[br].t[:, 0:n], in1=P[3 + br].t[:, 0:n], op=ALU.mult),
                      rd=[gt[br].r, P[3 + br].r], wr=[mt[br].r])
            kb.op("dve", lambda e: e.tensor_tensor(out=mt[0].t[:, 0:n], in0=mt[0].t[:, 0:n], in1=mt[1].t[:, 0:n], op=ALU.add),
                  rd=[mt[0].r, mt[1].r], wr=[mt[0].r])
            kb.op("dve", lambda e: e.tensor_tensor(out=mT.t[:, d, 0:n], in0=mt[0].t[:, 0:n], in1=mt[2].t[:, 0:n], op=ALU.add),
                  rd=[mt[0].r, mt[2].r], wr=[mT.rs[d]])
        kb.dma("pool", wo[0].t[:], wo_ap[0], wr=[wo[0].r])
        for d in range(16):
            if d + 1 < 16:
                kb.dma("pool", wo[(d + 1) % 2].t[:], wo_ap[d + 1], wr=[wo[(d + 1) % 2].r])
            wt = wo[d % 2]
            pt = P[d % 2]
            for kc in range(KC):
                kb.op("pe", lambda e: e.matmul(pt.t[:, 0:n], wt.t[:, kc, :], mT.t[:, kc, 0:n], start=(kc == 0), stop=(kc == KC - 1)),
                      rd=[wt.r, mT.rs[kc]], wr=[pt.r])
            kb.op("dve", lambda e: e.scalar_tensor_tensor(out=x.t[:, d, c0:c0 + n], in0=pt.t[:, 0:n], scalar=c.gsc.t[:, 1, d, w:w + 1],
                                                          in1=x.t[:, d, c0:c0 + n], op0=ALU.mult, op1=ALU.add),
                  rd=[pt.r, c.gsc.r, x.rs[d]], wr=[x.rs[d]])
    emit_ln(kb, c, x, 1, blks, lntmp)


def build_stage23(last, debug=False):
    nc = bass.Bass("TRN2", target_bir_lowering=False)
    if debug:
        dbg = {"yna": dram_out(nc, "d_yna", [128, 8, NT], BF16), "yf": dram_out(nc, "d_yf", [128, 4, NT], BF16),
               "yc": dram_out(nc, "d_yc", [128, 4, NT], BF16), "x2": dram_out(nc, "d_x2", [128, KC, NT])}
    l = 1 if last else 0
    xin = dram_in(nc, "x1T", [128, KC, NT])
    mod = dram_in(nc, "mod", [DEPTH, 128, 9 * KC * 2])
    lnp = dram_in(nc, "lnp", [DEPTH, 128, 6 * KC])
    at = {"qT": dram_in(nc, "qT", [128, 8, NT], BF16), "kTh": dram_in(nc, "kTh", [128, 8, NKH], BF16),
          "Vh": dram_in(nc, "Vh", [8, 128, 14, 128], BF16), "bias": dram_in(nc, "bias", [8, 128, NBIAS]),
          "ident": dram_in(nc, "ident", [128, 128])}
    ff = {"Xall": dram_in(nc, "Xall", [SEQ, 1024], BF16), "cn": dram_in(nc, "cn", [32, 128, 2, 1024], BF16),
          "Xrev": dram_in(nc, "Xrev", [SEQ // 2, 1024], BF16), "xnyq": dram_in(nc, "xnyq", [1, 1024], BF16),
          "alt": dram_in(nc, "alt", [1, 1024], BF16),
          "Xctx": dram_in(nc, "Xctx", [CTX, 1024], BF16), "cnc": dram_in(nc, "cnc", [2, 128, 2, TCX], BF16)}
    cv = {"uTh": dram_in(nc, "uTh", [128, 4, TL + 30]), "uTc": dram_in(nc, "uTc", [128, 4, TCX + 30]),
          "wdw": dram_in(nc, "wdw", [128, 4, 31]), "cvp": dram_in(nc, "cvp", [128, 4, 3])}
    wgb = dram_in(nc, "wgb", [16, 128, 64, 128], BF16)
    wo = dram_in(nc, "wo", [16, 128, KC, 128], BF16)
    bg = dram_in(nc, "bg", [128, 3, KC])
    f2w13 = dram_in(nc, "f2w13", [2, 22, 128, KC, 256], BF16)
    f2w2 = dram_in(nc, "f2w2", [8, 4, 128, 11, 256], BF16)
    if not last:
        f1w13 = dram_in(nc, "f1w13", [2, 22, 128, KC, 256], BF16)
        f1w2 = dram_in(nc, "f1w2", [8, 4, 128, 11, 256], BF16)
        winf = dram_in(nc, "winf", [14, 128, KC, 256], BF16)
        winv = dram_in(nc, "winv", [2, 128, KC, 512], BF16)
        cs128 = dram_in(nc, "cs128", [128, 256])
        outs = decl_proj_outs(nc)
    xout = dram_out(nc, "xoT", [128, KC, NT])
    wc = not last
    with ExitStack() as es:
        kb = KB(nc, es)
        c = setup_common(kb)
        x = kb.sb("x", [128, KC, NT], F32, nres=KC)
        kb.dma("sp", x.t[:], xin, wr=x.rs)
        load_mod(kb, c, mod, lnp, l)
        with ExitStack() as es2:
            kb.es = es2
            ynaT = kb.sb("ynaT", [128, 8, NT], BF16, nres=8)
            yfT = kb.sb("yfT", [128, 4, NT], BF16, nres=4)
            ycT = kb.sb("ycT", [128, 4, NT], BF16, nres=4)
            with ExitStack() as es3:
                kb.es = es3
                alloc_pt(kb, c)
                if "attn" not in SKIP:
                    emit_attn(kb, c, at, ynaT, wc)
                kb.barrier()
                kb.es = es2
            alloc_p67(kb, c)
            lntmp = make_ln_tmp(kb)
            with ExitStack() as es3:
                kb.es = es3
                fin = emit_conv(kb, c, cv, ycT, wc, lntmp) if "conv" not in SKIP else None
                if "fft" not in SKIP:
                    emit_fft(kb, c, ff, yfT, wc)
                if fin is not None:
                    fin()
                kb.barrier()
                kb.es = es2
            with ExitStack() as es3:
                kb.es = es3
                if debug:
                    kb.dma("sp", dbg["yna"], ynaT.t[:], rd=ynaT.rs)
                    kb.dma("sp", dbg["yf"], yfT.t[:], rd=yfT.rs)
                    kb.dma("sp", dbg["yc"], ycT.t[:], rd=ycT.rs)
                if "merge" not in SKIP:
                    emit_merge(kb, c, x, ynaT, yfT, ycT, wgb, wo, bg, wc, lntmp)
                if debug:
                    kb.dma("sp", dbg["x2"], x.t[:], rd=x.rs)
                kb.barrier()
                kb.es = es2
            kb.barrier()
            kb.es = es
        with ExitStack() as es2:
            kb.es = es2
            alloc_p67(kb, c)
            with ExitStack() as es3:
                kb.es = es3
                bufs = make_ffn_bufs(kb)
                passes = FFN_PASSES if wc else [[(0, 512)], [(512, 512)]]
                if "ffn2" not in SKIP:
                    emit_ffn(kb, c, x, 2, f2w13, f2w2, passes, bufs)
                kb.barrier()
                kb.es = es2
            if not last:
                load_mod(kb, c, mod, lnp, 1)
                with ExitStack() as es3:
                    kb.es = es3
                    bufs = make_ffn_bufs_named(kb, "b")
                    emit_ffn(kb, c, x, 0, f1w13, f1w2, FFN_PASSES, bufs)
                    kb.barrier()
                    kb.es = es2
                kb.dma("sp", xout, x.t[:], rd=x.rs)
                emit_proj(kb, c, x, winf, winv, cs128, outs)
            else:
                kb.dma("sp", xout, x.t[:], rd=x.rs)
            kb.barrier()
            kb.es = es
        kb.finish()
    return nc


def make_ffn_bufs_named(kb, sfx):
    b = {}
    b["h"] = kb.sb("ffn_h" + sfx, [128, KC, 544], BF16, nres=KC)
    b["g"] = kb.sb("ffn_g" + sfx, [128, NJ, 544], BF16, nres=NJ)
    b["w13"] = [[kb.sb("w13_%d_%d%s" % (s, i, sfx), [128, KC, 256], BF16) for i in range(2)] for s in range(2)]
    b["w2"] = [kb.sb("w2_%d%s" % (i, sfx), [128, 11, 256], BF16) for i in range(2)]
    b["sa"] = [kb.sb("sa%d%s" % (i, sfx), [128, 544], F32) for i in range(2)]
    b["ln"] = {"sq": [kb.sb("ln_sq%d%s" % (i, sfx), [128, 512], F32) for i in range(2)],
               "t1": [kb.sb("ln_t1%d%s" % (i, sfx), [128, 512], F32) for i in range(2)]}
    for nm in ("mean", "var", "rstd", "nmr"):
        b["ln"][nm] = kb.sb("ln_" + nm + sfx, [128, 512], F32)
    return b


def build_bias(rpb_l, i):
    out = np.full((8, 128, NBIAS), NEG, np.float32)
    cq = np.arange(64)
    c0 = np.clip(cq - 8, 0, 48)
    col_ok = (cq[None, :] >= c0[:, None]) & (cq[None, :] < c0[:, None] + 16)
    dc_i = np.clip(cq[None, :] - cq[:, None], -15, 15) + 15
    for lt in (0, 1, 2, 6, 7):
        blo, nrow, boff = ATT_WIN[lt]
        for qr in range(2):
            r = 16 * i + 2 * lt + qr
            r0 = min(max(r - 4, 0), 120)
            for bb in range(nrow):
                kr = 16 * i - 4 + blo + bb
                if 0 <= kr < 128 and r0 <= kr < r0 + 8:
                    vals = rpb_l[:, kr - r + 7, :][:, dc_i]
                    out[:, 64 * qr:64 * qr + 64, boff + 64 * bb:boff + 64 * bb + 64] = np.where(col_ok[None], vals, np.float32(NEG))
        out[:, :, boff + 64 * nrow:boff + 64 * nrow + CTX] = 0.0
    return out


def dft_tables(i):
    t = np.arange(SEQ // 2, dtype=np.int64)[:, None]
    k = (1024 * i + np.arange(1024, dtype=np.int64))[None, :]
    ang = 2.0 * np.pi * ((t * k) % SEQ).astype(np.float64) / SEQ
    cn = np.stack([np.cos(ang), -np.sin(ang)], axis=1) / np.sqrt(float(SEQ))
    cn = np.ascontiguousarray(cn.reshape(32, 128, 2, 1024).astype(np.float32).astype(NPBF))
    alt = np.where((1024 * i + np.arange(1024)) % 2 == 0, 1.0, -1.0) / np.sqrt(float(SEQ))
    alt = np.ascontiguousarray(alt.reshape(1, 1024).astype(np.float32).astype(NPBF))
    t = np.arange(CTX, dtype=np.int64)[:, None]
    k = (TCX * i + np.arange(TCX, dtype=np.int64))[None, :]
    ang = 2.0 * np.pi * ((t * k) % CTX).astype(np.float64) / CTX
    cc = np.stack([np.cos(ang), -np.sin(ang)], axis=1) / np.sqrt(float(CTX))
    cc = np.ascontiguousarray(cc.reshape(2, 128, 2, TCX).astype(np.float32).astype(NPBF))
    return cn, cc, alt


def exchange(res, l, inp, consts):
    R = [{k: np.asarray(v) for k, v in r.items()} for r in res]
    kT = np.concatenate([r["o_kT"][:, :, :TL] for r in R], axis=2)
    kTc = np.concatenate([r["o_kT"][:, :, TL:] for r in R], axis=2)
    kpad = np.concatenate([np.zeros((128, 8, 256), kT.dtype), kT, np.zeros((128, 8, 256), kT.dtype)], axis=2)
    V = np.concatenate([r["o_V"][:TL] for r in R], axis=0)
    Vc = np.concatenate([r["o_V"][TL:] for r in R], axis=0)
    Vpad = np.concatenate([np.zeros((256, 1024), V.dtype), V, np.zeros((256, 1024), V.dtype)], axis=0)
    Xall = np.ascontiguousarray(np.concatenate([r["o_XcXs"][:TL] for r in R], axis=0))
    Xctx = np.ascontiguousarray(np.concatenate([r["o_XcXs"][TL:] for r in R], axis=0))
    Xrev = np.ascontiguousarray(np.concatenate([np.zeros((1, 1024), Xall.dtype), Xall[:SEQ // 2:-1]], axis=0))
    assert Xrev.shape[0] == SEQ // 2
    xnyq = np.ascontiguousarray(Xall[SEQ // 2:SEQ // 2 + 1])
    uT = np.concatenate([r["o_uT"][:, :, :TL] for r in R], axis=2)
    uTc = np.concatenate([r["o_uT"][:, :, TL:] for r in R], axis=2)
    upad = np.concatenate([np.zeros((128, 4, 15), uT.dtype), uT, np.zeros((128, 4, 15), uT.dtype)], axis=2)
    ucpad = np.concatenate([np.zeros((128, 4, 15), uT.dtype), uTc, np.zeros((128, 4, 15), uT.dtype)], axis=2)
    wdw = np.ascontiguousarray(inp["w_dw"][l].T.reshape(4, 128, 31).transpose(1, 0, 2))
    cvp = np.ascontiguousarray(np.stack([inp["b_dw"][l], inp["conv_ln_g"][l], inp["conv_ln_b"][l]], axis=-1).reshape(4, 128, 3).transpose(1, 0, 2))
    wb = consts["wb"][l]
    bgt = np.ascontiguousarray(inp["b_gate"][l].reshape(3, KC, 128).transpose(2, 0, 1))
    maps = []
    for i in range(NCORE):
        cn, cc, alt = consts["dft"][i]
        Vh = np.concatenate([Vpad[1024 * i:1024 * i + 1536], Vc], axis=0).reshape(14, 128, 8, 128).transpose(2, 1, 0, 3)
        m = {"x1T": R[i]["x1T"] if "x1T" in R[i] else R[i]["xoT"], "mod": consts["mod"], "lnp": consts["lnp"],
             "qT": R[i]["o_qT"], "kTh": np.ascontiguousarray(np.concatenate([kpad[:, :, 1024 * i:1024 * i + 1536], kTc], axis=2)),
             "Vh": np.ascontiguousarray(Vh), "bias": build_bias(inp["rpb"][l], i), "ident": np.eye(128, dtype=np.float32),
             "Xall": Xall, "cn": cn, "Xctx": Xctx, "cnc": cc, "Xrev": Xrev, "xnyq": xnyq, "alt": alt,
             "uTh": np.ascontiguousarray(upad[:, :, 1024 * i:1024 * i + TL + 30]),
             "uTc": np.ascontiguousarray(ucpad[:, :, TCX * i:TCX * i + TCX + 30]),
             "wdw": wdw, "cvp": cvp, "wgb": wb["wgb"], "wo": wb["wo"], "bg": bgt, "f2w13": wb["f2w13"], "f2w2": wb["f2w2"]}
        maps.append(m)
    return maps


def kernel(**inp):
    inp = {k: np.asarray(v) for k, v in inp.items()}
    cores = list(range(NCORE))
    modT, wb = run_mod(inp["c"][0], inp["c_ctx"], inp["w_ada"], inp["b_ada"], inp)
    consts = {"mod": modT, "wb": wb, "lnp": make_lnp(inp["ln_g"], inp["ln_b"]), "dft": [dft_tables(i) for i in range(NCORE)]}
    x = inp["x"][0]
    ctx = inp["ctx"][0]
    cs128 = dft_cs128()

    def nxt(l):
        return {"f1w13": wb[l]["f1w13"], "f1w2": wb[l]["f1w2"], "winf": wb[l]["winf"], "winv": wb[l]["winv"], "cs128": cs128}
    n0 = nxt(0)
    maps = []
    for i in cores:
        toks = np.concatenate([x[TL * i:TL * (i + 1)], ctx[TCX * i:TCX * (i + 1)]], axis=0)
        m = {"xT": np.ascontiguousarray(to_fm(toks)), "mod": consts["mod"], "lnp": consts["lnp"]}
        m.update(n0)
        maps.append(m)
    res = run_bass_kernel_spmd(build_stage1(), maps, core_ids=cores).results
    del n0, maps
    maps = exchange(res, 0, inp, consts)
    n1 = nxt(1)
    for m in maps:
        m.update(n1)
    res = run_bass_kernel_spmd(build_stage23(False), maps, core_ids=cores).results
    del n1, maps
    maps = exchange(res, 1, inp, consts)
    res = run_bass_kernel_spmd(build_stage23(True), maps, core_ids=cores).results
    out = np.zeros((1, SEQ, D), np.float32)
    for i in cores:
        xo = np.asarray(res[i]["xoT"])[:, :, :TL]
        out[0, TL * i:TL * (i + 1)] = xo.transpose(2, 1, 0).reshape(TL, D)
    return out
```
